# Optimizing a Trainium2 kernel written in Bass

```python
import jax, jax.numpy as jnp
from jax import lax
import numpy as np

D_MODEL = 1024
BATCH = 8
SEQ = 4096
DEPTH = 1

D_MIX = D_MODEL
D_POOL = D_MIX // 2
D_DN = D_MIX - D_POOL
POOL_WINDOWS = (2, 4, 8, 16)
N_POOL_GROUPS = len(POOL_WINDOWS)
POOL_GROUP = D_POOL // N_POOL_GROUPS
DN_HEAD_DIM = 128
DN_HEADS = D_DN // DN_HEAD_DIM
CONV_WIDTH = 4
CHUNK = 64
NORM_EPS = 1e-6
SPLIT_SIZES = (D_POOL, D_POOL, D_DN, D_DN, D_DN, D_DN, DN_HEADS, DN_HEADS)
D_IN = sum(SPLIT_SIZES)

kernel_name = "hymba_pool_gated_deltanet_block"


def rms_norm(x, w):
    xf = x.astype(jnp.float32)
    y = xf * lax.rsqrt(jnp.mean(xf * xf, axis=-1, keepdims=True) + NORM_EPS)
    return (y * w.astype(jnp.float32)).astype(x.dtype)


def l2_normalize(t):
    return t * lax.rsqrt(jnp.sum(t * t, axis=-1, keepdims=True) + NORM_EPS)


def pool_mixer(u, z, pool_w, pool_scale):
    B, S, _ = u.shape
    uf = u.astype(jnp.float32).reshape(B, S, N_POOL_GROUPS, POOL_GROUP)
    csum = jnp.cumsum(uf, axis=1)
    counts = jnp.arange(1, S + 1, dtype=jnp.float32)
    outs = []
    for gi, w in enumerate(POOL_WINDOWS):
        c = csum[:, :, gi]
        prev = jnp.pad(c, ((0, 0), (w, 0), (0, 0)))[:, :S]
        cnt = jnp.minimum(counts, float(w))[None, :, None]
        outs.append((c - prev) / cnt)
    mix = jnp.stack(outs, axis=2) - uf
    mix = jnp.einsum('bsgc,gcd->bsgd', mix, pool_w.astype(jnp.float32)).reshape(B, S, D_POOL)
    out = mix * pool_scale.astype(jnp.float32) * jax.nn.silu(z.astype(jnp.float32))
    return out.astype(u.dtype)


def causal_depthwise_conv(u, w):
    K, C = w.shape
    return lax.conv_general_dilated(
        u, w[:, None, :], window_strides=(1,), padding=[(K - 1, 0)],
        dimension_numbers=('NWC', 'WIO', 'NWC'), feature_group_count=C)


def gated_delta_rule(q, k, v, g, beta):
    B, H, S, dk = q.shape
    dv = v.shape[-1]
    n = S // CHUNK
    q = q * (dk ** -0.5)
    k_beta = k * beta[..., None]
    v_beta = v * beta[..., None]
    chunked = lambda t: t.reshape(B, H, n, CHUNK, t.shape[-1])
    q, k, k_beta, v_beta = chunked(q), chunked(k), chunked(k_beta), chunked(v_beta)
    gc = jnp.cumsum(g.reshape(B, H, n, CHUNK), axis=-1)
    causal = jnp.tril(jnp.ones((CHUNK, CHUNK), dtype=bool))
    strict = jnp.tril(jnp.ones((CHUNK, CHUNK), dtype=bool), k=-1)
    diff = gc[..., :, None] - gc[..., None, :]
    decay = jnp.exp(jnp.where(causal, diff, -jnp.inf))
    A = jnp.where(strict, jnp.einsum('bhncd,bhnmd->bhncm', k_beta, k) * decay, 0.0)
    eye = jnp.eye(CHUNK, dtype=jnp.float32)
    T = lax.linalg.triangular_solve(A + eye, jnp.broadcast_to(eye, A.shape),
                                    left_side=True, lower=True, unit_diagonal=True)
    u = jnp.einsum('bhncm,bhnmd->bhncd', T, v_beta)
    w = jnp.einsum('bhncm,bhnmd->bhncd', T, k_beta * jnp.exp(gc)[..., None])
    qk = jnp.einsum('bhncd,bhnmd->bhncm', q, k) * decay
    q_dec = q * jnp.exp(gc)[..., None]
    k_dec = k * jnp.exp(gc[..., -1:] - gc)[..., None]
    chunk_decay = jnp.exp(gc[..., -1])

    def step(state, xs):
        qk_i, q_dec_i, k_dec_i, u_i, w_i, dec_i = xs
        v_new = u_i - jnp.einsum('bhcd,bhde->bhce', w_i, state)
        o = jnp.einsum('bhcd,bhde->bhce', q_dec_i, state) + jnp.einsum('bhcm,bhme->bhce', qk_i, v_new)
        state = state * dec_i[..., None, None] + jnp.einsum('bhcd,bhce->bhde', k_dec_i, v_new)
        return state, o

    to_scan = lambda t: jnp.moveaxis(t, 2, 0)
    xs = (to_scan(qk), to_scan(q_dec), to_scan(k_dec), to_scan(u), to_scan(w), to_scan(chunk_decay))
    state0 = jnp.zeros((B, H, dk, dv), dtype=jnp.float32)
    _, o = lax.scan(step, state0, xs)
    return jnp.moveaxis(o, 0, 2).reshape(B, H, S, dv)


def deltanet_mixer(q, k, v, z, b, a, conv_w, a_log, dt_bias, norm_w):
    B, S, _ = q.shape
    out_dtype = q.dtype
    qkv = jnp.concatenate([q, k, v], axis=-1).astype(jnp.float32)
    qkv = jax.nn.silu(causal_depthwise_conv(qkv, conv_w.astype(jnp.float32)))
    q, k, v = jnp.split(qkv, 3, axis=-1)
    heads = lambda t: t.reshape(B, S, DN_HEADS, DN_HEAD_DIM).transpose(0, 2, 1, 3)
    q, k, v = l2_normalize(heads(q)), l2_normalize(heads(k)), heads(v)
    beta = jax.nn.sigmoid(b.astype(jnp.float32)).transpose(0, 2, 1)
    g = (-jnp.exp(a_log.astype(jnp.float32))
         * jax.nn.softplus(a.astype(jnp.float32) + dt_bias.astype(jnp.float32))).transpose(0, 2, 1)
    o = gated_delta_rule(q, k, v, g, beta).transpose(0, 2, 1, 3)
    o = o * lax.rsqrt(jnp.mean(o * o, axis=-1, keepdims=True) + NORM_EPS) * norm_w.astype(jnp.float32)
    o = o * jax.nn.silu(z.astype(jnp.float32).reshape(B, S, DN_HEADS, DN_HEAD_DIM))
    return o.reshape(B, S, D_DN).astype(out_dtype)


def setup_inputs(seed: int = 0) -> dict:
    key = jax.random.key(seed)
    ks = jax.random.split(key, 12)
    f32 = jnp.float32
    x = jax.random.normal(ks[0], (BATCH, SEQ, D_MODEL), f32)
    norm_w = 1.0 + 0.02 * jax.random.normal(ks[1], (DEPTH, D_MODEL), f32)
    w_in = jax.random.normal(ks[2], (DEPTH, D_MODEL, D_IN), f32) * D_MODEL ** -0.5
    pool_w = jax.random.normal(ks[3], (DEPTH, N_POOL_GROUPS, POOL_GROUP, POOL_GROUP), f32) * POOL_GROUP ** -0.5
    pool_scale = 1.0 + 0.1 * jax.random.normal(ks[4], (DEPTH, D_POOL), f32)
    conv_w = jax.random.normal(ks[5], (DEPTH, CONV_WIDTH, 3 * D_DN), f32) * CONV_WIDTH ** -0.5
    a_log = jnp.log(jax.random.uniform(ks[6], (DEPTH, DN_HEADS), f32, 1.0, 16.0))
    dt = jnp.exp(jax.random.uniform(ks[7], (DEPTH, DN_HEADS), f32, np.log(1e-3), np.log(1e-1)))
    dt_bias = dt + jnp.log(-jnp.expm1(-dt))
    dn_norm_w = 1.0 + 0.02 * jax.random.normal(ks[8], (DEPTH, DN_HEAD_DIM), f32)
    w_out = jax.random.normal(ks[9], (DEPTH, D_MIX, D_MODEL), f32) * D_MIX ** -0.5
    final_norm_w = 1.0 + 0.02 * jax.random.normal(ks[10], (D_MODEL,), f32)
    return {"x": x, "norm_w": norm_w, "w_in": w_in, "pool_w": pool_w, "pool_scale": pool_scale,
            "conv_w": conv_w, "a_log": a_log, "dt_bias": dt_bias, "dn_norm_w": dn_norm_w,
            "w_out": w_out, "final_norm_w": final_norm_w}


def reference(x, norm_w, w_in, pool_w, pool_scale, conv_w, a_log, dt_bias, dn_norm_w, w_out, final_norm_w):
    offsets = np.cumsum((0,) + SPLIT_SIZES)
    h = x
    for layer in range(DEPTH):
        n = rms_norm(h, norm_w[layer])
        proj = jnp.einsum('bsd,de->bse', n, w_in[layer])
        pu, pz, q, k, v, dz, b, a = [proj[..., int(offsets[i]):int(offsets[i + 1])]
                                     for i in range(len(SPLIT_SIZES))]
        y_pool = pool_mixer(pu, pz, pool_w[layer], pool_scale[layer])
        y_dn = deltanet_mixer(q, k, v, dz, b, a, conv_w[layer], a_log[layer],
                              dt_bias[layer], dn_norm_w[layer])
        y = jnp.concatenate([y_pool, y_dn], axis=-1)
        h = h + jnp.einsum('bse,ed->bsd', y, w_out[layer])
    return rms_norm(h, final_norm_w)
```

```python
import contextlib
import numpy as np
import ml_dtypes
import concourse.bass as bass
import concourse.mybir as mybir
from concourse.bass_utils import run_bass_kernel_spmd

F32 = mybir.dt.float32
BF16 = mybir.dt.bfloat16
AF = mybir.ActivationFunctionType
ALU = mybir.AluOpType
AX = mybir.AxisListType

SEQ = 4096
D = 1024
DIN = 3080
NT = SEQ // 128
EPS = 1e-6
POOL_WINDOWS = (2, 4, 8, 16)
C_PU, C_PZ, C_Q, C_K, C_V, C_DZ, C_BA = 0, 512, 1024, 1536, 2048, 2560, 3072


class _Op:
    __slots__ = ("eng", "fn", "reads", "writes", "dma_key", "idx", "deps", "sig", "dma_val", "waits", "cost", "tset")

    def __init__(self, eng, fn, reads, writes, dma_key=None):
        self.eng = eng
        self.fn = fn
        self.reads = tuple(reads)
        self.writes = tuple(writes)
        self.dma_key = dma_key
        self.sig = None
        self.dma_val = None
        self.cost = 100.0
        self.tset = None


class Sched:
    COMPUTE = ("pe", "act", "dve", "pool")
    XLAT = 250.0
    NO_WAR = 0
    ACT_PEN = 1400.0
    JITTER = None
    ATTACH_WAIT = False
    JUNK = None
    PS_BIAS = 0.0

    def __init__(self, nc):
        self.nc = nc
        self.ops = []
        self.cuts = []

    def cut(self):
        self.cuts.append(len(self.ops))

    def op(self, eng, fn, reads=(), writes=()):
        extra = [r for r in reads if r.startswith("ps") and r not in writes]
        if extra:
            writes = list(writes) + extra
        o = _Op(eng, fn, reads, writes)
        self.ops.append(o)
        return o

    def dma(self, queue, out, in_, reads, writes, key):
        o = _Op(queue, lambda e: e.dma_start(out=out, in_=in_), reads, writes, dma_key=key)
        nb = 4
        for d_ in out.shape:
            nb *= int(d_)
        o.cost = 2200.0 + nb / 200.0
        self.ops.append(o)
        return o

    def schedule(self):
        ops = self.ops
        n = len(ops)
        last_w, readers = {}, {}
        preds = [set() for _ in range(n)]
        for i, o in enumerate(ops):
            for r in o.reads:
                w = last_w.get(r)
                if w is not None:
                    preds[i].add(w)
            for wk in o.writes:
                w = last_w.get(wk)
                if w is not None:
                    preds[i].add(w)
                if not (Sched.NO_WAR and (Sched.NO_WAR == 2 or not wk.startswith("ps"))):
                    for rd in readers.get(wk, ()):
                        preds[i].add(rd)
            preds[i].discard(i)
            for r in o.reads:
                readers.setdefault(r, set()).add(i)
            for wk in o.writes:
                last_w[wk] = i
                readers[wk] = set()
        succs = [[] for _ in range(n)]
        npred = [len(p) for p in preds]
        for i, p in enumerate(preds):
            for j in p:
                succs[j].append(i)
        free = {}
        rtime = [0.0] * n
        ready = set(i for i in range(n) if npred[i] == 0)
        order = []
        self.trace = []
        jit = None
        if Sched.JITTER is not None:
            import random as _random
            rng_ = _random.Random(Sched.JITTER[0])
            jit = [rng_.uniform(0.0, Sched.JITTER[1]) for _ in range(n)]
        open_ps = {}
        self.bind = {}
        self.seq_ops = list(ops)
        act_set = None
        XLAT = Sched.XLAT
        while ready:
            best, bk = None, None
            for i in ready:
                o = ops[i]
                st = max(free.get(o.eng, 0.0), rtime[i])
                if o.eng == "act" and o.tset is not None and o.tset != act_set:
                    st += Sched.ACT_PEN
                pr = st + (jit[i] if jit is not None else 0.0)
                if Sched.PS_BIAS:
                    for kk_ in o.writes:
                        if kk_.startswith("ps") and open_ps.get(kk_, 0) > 0:
                            pr = st - Sched.PS_BIAS
                            break
                k = (pr, i, st)
                if bk is None or k < bk:
                    best, bk = i, k
            i = best
            o = ops[i]
            ready.discard(i)
            st = bk[2]
            if o.eng == "pe" and Sched.JUNK is not None:
                gap_ = st - free.get("pe", 0.0)
                if gap_ > Sched.JUNK[0] and free.get("pe", 0.0) > 0.0:
                    for _ in range(min(int(gap_ * Sched.JUNK[1] / 60.0), Sched.JUNK[2])):
                        order.append(Sched.JUNK[3]())
            for kk_ in o.writes:
                if kk_.startswith("ps"):
                    if o.eng == "pe":
                        open_ps[kk_] = open_ps.get(kk_, 0) + 1
                    else:
                        open_ps[kk_] = 0
            if o.eng == "act" and o.tset is not None:
                act_set = o.tset
            if o.dma_key is not None:
                free[o.eng] = st + 60.0
                fin = st + o.cost
            else:
                fin = st + o.cost
                free[o.eng] = fin
            order.append(o)
            self.trace.append((i, o.eng, st, fin, rtime[i], o.cost))
            for j in succs[i]:
                lat = XLAT if (ops[j].eng != o.eng or o.dma_key is not None) else 60.0
                if fin + lat > rtime[j]:
                    rtime[j] = fin + lat
                    self.bind[j] = i
                npred[j] -= 1
                if npred[j] == 0:
                    ready.add(j)
        self.ops = order
        self.est_total = max(free.values()) if free else 0.0

    def emit(self, sems):
        nc = self.nc
        last_w = {}
        readers = {}
        dma_cnt = {}
        for i, o in enumerate(self.ops):
            o.idx = i
            deps = set()
            for r in o.reads:
                w = last_w.get(r)
                if w is not None:
                    deps.add(w)
            for wk in o.writes:
                w = last_w.get(wk)
                if w is not None:
                    deps.add(w)
                for rd in readers.get(wk, ()):
                    deps.add(rd)
            deps.discard(o)
            o.deps = deps
            for r in o.reads:
                readers.setdefault(r, set()).add(o)
            for wk in o.writes:
                last_w[wk] = o
                readers[wk] = set()
            if o.dma_key is not None:
                dma_cnt[o.dma_key] = dma_cnt.get(o.dma_key, 0) + 1
                o.dma_val = 16 * dma_cnt[o.dma_key]
        needed = set()
        for o in self.ops:
            best = {}
            dma_deps = []
            for d in o.deps:
                if d.dma_key is not None:
                    dma_deps.append(d)
                    continue
                if d.eng == "pe" and o.eng == "pe":
                    continue
                b = best.get(d.eng)
                if b is None or d.idx > b.idx:
                    best[d.eng] = d
            o.deps = list(best.values()) + dma_deps
            for d in best.values():
                needed.add(d)
        cnt = {e: 0 for e in self.COMPUTE}
        for o in self.ops:
            if o in needed:
                cnt[o.eng] += 1
                o.sig = cnt[o.eng]
        waited = {}
        for o in self.ops:
            w = []
            for d in o.deps:
                if d.dma_key is not None:
                    k = ("dma", d.dma_key)
                    v = d.dma_val
                    s = sems[d.dma_key]
                else:
                    k = d.eng
                    v = d.sig
                    s = sems[d.eng]
                cur = waited.setdefault(o.eng, {}).get(k, 0)
                if cur >= v:
                    continue
                waited[o.eng][k] = v
                w.append((s, v))
            o.waits = w
        final_dma = {k: 16 * c for k, c in dma_cnt.items()}
        self.sig_counts = cnt
        by_eng = {}
        for o in self.ops:
            by_eng.setdefault(o.eng, []).append(o)
        engmap = {"pe": "tensor", "act": "scalar", "dve": "vector", "pool": "gpsimd", "sp": "sync"}

        def make(engname, ops):
            def body(eng):
                for o in ops:
                    wl = o.waits
                    attach = None
                    if Sched.ATTACH_WAIT and wl and o.dma_key is None:
                        attach = wl[-1]
                        wl = wl[:-1]
                    for (s, v) in wl:
                        eng.wait_ge(s, v)
                    ins = o.fn(eng)
                    if attach is not None:
                        ins._wait_ge(attach[0], attach[1])
                    if o.dma_key is not None:
                        ins.then_inc(sems[o.dma_key], 16)
                    elif o.sig is not None:
                        ins.then_inc(sems[o.eng], 1)
                if engname == "sp":
                    for k, v in final_dma.items():
                        eng.wait_ge(sems[k], v)
            return body

        with nc.Block() as block:
            for engname in ("sp", "pe", "act", "dve", "pool"):
                ops = by_eng.get(engname, [])
                if not ops and engname != "sp":
                    continue
                getattr(block, engmap[engname])(make(engname, ops))


def _consts():
    c = {}
    idx = np.arange(128)
    same = (idx[:, None] // 64) == (idx[None, :] // 64)
    c["ident32"] = np.eye(128, dtype=np.float32)
    c["u2"] = (same & (idx[:, None] <= idx[None, :])).astype(np.float32)
    c["u2rep"] = np.tile(c["u2"], (1, 4))
    c["cisame"] = same.astype(np.float32)
    c["ci0"] = np.repeat((idx < 64).astype(np.float32)[:, None], 128, axis=1)
    c["ci1"] = np.repeat((idx >= 64).astype(np.float32)[:, None], 128, axis=1)
    ms = (same & (idx[None, :] < idx[:, None])).astype(np.float32)
    mu = (same & (idx[None, :] >= idx[:, None])).astype(np.float32)
    c["ms"] = np.tile(ms, (1, 4))
    c["mu"] = np.tile(mu, (1, 4))
    c["ones32"] = np.ones((128, 1), np.float32)
    wct = np.zeros((128, 4, 128), np.float32)
    wpt = np.zeros((128, 4, 128), np.float32)
    wc0 = np.zeros((128, 4, 128), np.float32)
    for g, w in enumerate(POOL_WINDOWS):
        for t in range(128):
            for j in range(t - w + 1, t + 1):
                if j >= 0:
                    wct[j, g, t] += 1.0 / w
                else:
                    wpt[128 + j, g, t] += 1.0 / w
            wct[t, g, t] -= 1.0
            cnt = min(t + 1, w)
            for j in range(max(0, t - w + 1), t + 1):
                wc0[j, g, t] += 1.0 / cnt
            wc0[t, g, t] -= 1.0
    corr = wc0 - wct
    hi = corr.astype(ml_dtypes.bfloat16).astype(np.float32)
    lo = (corr - hi).astype(ml_dtypes.bfloat16).astype(np.float32)
    c["wct"] = wct.reshape(128, 512)
    c["wpt"] = wpt.reshape(128, 512)
    c["wc0hi"] = hi.reshape(128, 512)
    c["wc0lo"] = lo.reshape(128, 512)
    return c


CONST_SHAPES = {"ident32": [128, 128], "u2": [128, 128], "u2rep": [128, 512], "cisame": [128, 128], "ci0": [128, 128],
                "ci1": [128, 128], "ms": [128, 512], "mu": [128, 512], "ones32": [128, 1],
                "wct": [128, 512], "wpt": [128, 512], "wc0hi": [128, 512], "wc0lo": [128, 512]}


def build_nc(n_tiles=NT, dbg_tile=None, cfg=None):
    cfg = cfg or {}
    nc = bass.Bass("TRN2", target_bir_lowering=False)
    dram = {}

    def din(name, shape):
        dram[name] = nc.dram_tensor(name, list(shape), F32, kind="ExternalInput").ap()
        return dram[name]

    x_d = din("x", [SEQ, D])
    win_d = din("w_in", [D, DIN])
    wout_d = din("w_out", [D, D])
    poolw_d = din("pool_w", [128, 4, 128])
    psrow_d = din("pool_scale", [512])
    convw_d = din("conv_w", [128, 48])
    alog_d = din("a_log", [4])
    dtb_d = din("dt_bias", [4])
    dnwc_d = din("dn_norm_w", [128, 1])
    nw_d = din("norm_w", [128, 8])
    fnw_d = din("final_norm_w", [D])
    cd = {k: din("c_" + k, s) for k, s in CONST_SHAPES.items()}
    out_d = nc.dram_tensor("out", [SEQ, D], F32, kind="ExternalOutput").ap()
    dbg_out = {}

    S = Sched(nc)
    with contextlib.ExitStack() as es:
        def sb(name, shape, dt=F32):
            return es.enter_context(nc.sbuf_tensor(name, list(shape), dt))

        def psum(name, shape, dt):
            return es.enter_context(nc.psum_tensor(name, list(shape), dt))

        sems = {}

        def sem(k):
            if k not in sems:
                sems[k] = es.enter_context(nc.semaphore("s_" + k))
            return k

        for e in ("pe", "act", "dve", "pool"):
            sem(e)

        def act(out, in_, func, reads, writes, bias=None, scale=None, accum=None):
            kw = {}
            if bias is not None:
                kw["bias"] = bias
            if scale is not None:
                kw["scale"] = scale
            if accum is not None:
                kw["accum_out"] = accum
            o_ = S.op("act", lambda e: e.activation(out=out, in_=in_, func=func, **kw), reads, writes)
            o_.cost = 190.0 + 0.83 * fsz(out)
            o_.tset = {AF.Silu: "silu", AF.Exp: "lnexp", AF.Ln: "lnexp"}.get(func)
            return o_

        def fsz(ap):
            n_ = 1
            for d_ in ap.shape[1:]:
                n_ *= int(d_)
            return n_

        def ecost(eng, out):
            n_ = fsz(out)
            if eng == "act":
                return 190.0 + 0.83 * n_
            if eng == "pool":
                return 120.0 + 2.2 * n_
            return 110.0 + 0.8 * n_

        def cp(eng, out, in_, reads, writes):
            if eng == "act":
                o_ = S.op("act", lambda e: e.copy(out=out, in_=in_), reads, writes)
            else:
                o_ = S.op(eng, lambda e: e.tensor_copy(out=out, in_=in_), reads, writes)
            o_.cost = ecost(eng, out)
            return o_

        def tt(eng, out, in0, in1, op, reads, writes):
            o_ = S.op(eng, lambda e: e.tensor_tensor(out=out, in0=in0, in1=in1, op=op), reads, writes)
            o_.cost = ecost(eng, out) + (800.0 if op == ALU.pow else 0.0)
            return o_

        def stt(eng, out, in0, scalar, in1, op0, op1, reads, writes):
            o_ = S.op(eng, lambda e: e.scalar_tensor_tensor(out=out, in0=in0, scalar=scalar, in1=in1,
                                                            op0=op0, op1=op1), reads, writes)
            o_.cost = ecost(eng, out) * 1.1
            return o_

        def ts(eng, out, in0, s1, s2, op0, op1, reads, writes):
            if s2 is None:
                o_ = S.op(eng, lambda e: e.tensor_scalar(out=out, in0=in0, scalar1=s1, scalar2=None, op0=op0),
                          reads, writes)
            else:
                o_ = S.op(eng, lambda e: e.tensor_scalar(out=out, in0=in0, scalar1=s1, scalar2=s2, op0=op0, op1=op1),
                          reads, writes)
            o_.cost = ecost(eng, out)
            return o_

        def mm(out, lhsT, rhs, reads, writes, start=True, stop=True, skip=False):
            o_ = S.op("pe", lambda e: e.matmul(out, lhsT=lhsT, rhs=rhs, start=start, stop=stop,
                                               skip_group_check=skip), reads, writes)
            n_ = fsz(out)
            o_.cost = max(n_, 128) * 0.62 + 5.0
            if int(out.shape[0]) < 128:
                o_.cost = max(o_.cost, 160.0)
            if lhsT.dtype == F32:
                o_.cost = 4.0 * max(n_, 64) / 1.8 + 90.0
            return o_

        def tr(out, in_, reads, writes):
            o_ = S.op("pe", lambda e: e.transpose(out=out, in_=in_, identity=ID16[:]), list(reads) + ["ID16"], writes)
            o_.cost = 82.0
            return o_

        def bc4(ap4):
            return ap4.unsqueeze(2).to_broadcast([128, 4, 128])

        def v4(ap512):
            return ap512.rearrange("p (h d) -> p h d", h=4)

        PSF = [psum(f"psf{i}", [128, 512], F32) for i in range(8)]
        pools = cfg.get("pools", {"F": [0, 1, 2, 3], "B": [4, 5], "C": [6, 7]})
        pcnt = {"F": 0, "B": 0, "C": 0}
        stage_ = ["F"]

        def _next_bank():
            ids = pools[stage_[0]]
            i = ids[pcnt[stage_[0]] % len(ids)]
            pcnt[stage_[0]] += 1
            key = f"psf{i}"
            if cfg.get("inf_psum"):
                key = f"psf{i}_u{pcnt[stage_[0]]}{stage_[0]}"
            return i, key

        def pf():
            i, key = _next_bank()
            return PSF[i], key

        def pb():
            i, key = _next_bank()
            return PSF[i][:, :].bitcast(BF16), key

        WIN16 = sb("WIN16", [128, 8, DIN], BF16)
        WOUT16 = sb("WOUT16", [128, 8, D], BF16)
        POOLW16 = sb("POOLW16", [128, 4, 128], BF16)
        DIAG16 = sb("DIAG16", [128, 48, 128], BF16)
        NWC = sb("NWC", [128, 8])
        FNW = sb("FNW", [128, D])
        DNWC = sb("DNWC", [128, 1])
        PSROW = sb("PSROW", [128, 512])
        CONVW = sb("CONVW", [128, 48])
        ALOG = sb("ALOG", [128, 4])
        DTB = sb("DTB", [128, 4])
        NEGA = sb("NEGA", [128, 4])
        EPSC = sb("EPSC", [128, 1])
        ONEC = sb("ONEC", [128, 1])
        ONES16 = sb("ONES16", [128, 1], BF16)
        ONES128 = sb("ONES128", [128, 128])
        STAGED = {"ms": "MS16", "mu": "MU16", "wct": "WCT16", "wpt": "WPT16", "wc0hi": "WC0HI16", "wc0lo": "WC0LO16"}
        C32 = {k: sb("C_" + k, s) for k, s in CONST_SHAPES.items() if k not in STAGED}
        ID16 = sb("ID16", [128, 128], BF16)
        MS16 = sb("MS16", [128, 512], BF16)
        MU16 = sb("MU16", [128, 512], BF16)
        WCT16 = sb("WCT16", [128, 512], BF16)
        WPT16 = sb("WPT16", [128, 512], BF16)
        WC0HI16 = sb("WC0HI16", [128, 512], BF16)
        WC0LO16 = sb("WC0LO16", [128, 512], BF16)

        X32 = [sb(f"X32_{i}", [128, D]) for i in range(3)]
        H32 = sb("H32", [128, D])
        OUT32 = [sb(f"OUT32_{i}", [128, D]) for i in range(2)]
        XN16 = sb("XN16", [128, D], BF16)
        XT16 = sb("XT16", [128, 8, 128], BF16)
        RAW16 = [sb(f"RAW16_{i}", [128, 12, 131], BF16) for i in range(2)]
        QKV16 = sb("QKV16", [128, 12, 128], BF16)
        SQ16 = sb("SQ16", [128, 8, 128], BF16)
        SZ16 = sb("SZ16", [128, 512], BF16)
        U16 = [sb(f"U16_{i}", [128, 512], BF16) for i in range(2)]
        MIX16 = sb("MIX16", [128, 512], BF16)
        NHALF = sb("NHALF", [128, 8])
        GU = sb("GU", [128, 512])
        DA = sb("DA", [128, 512])
        EE = sb("EE16", [128, 512], BF16)
        AROW = sb("AROW", [128, 512], BF16)
        GSB = sb("GSB", [128, 512], BF16)
        GT = sb("GT", [128, 512], BF16)
        SMF = [sb(f"SMF_{i}", [128, 128]) for i in range(3)]
        GATE = [sb(f"GATE_{i}", [128, 512], BF16) for i in range(3)]
        KBA16 = [sb(f"KBA16_{i}", [128, 512], BF16) for i in range(2)]
        KDEC16 = [sb(f"KDEC16_{i}", [128, 512], BF16) for i in range(3)]
        VB16 = [sb(f"VB16_{i}", [128, 512], BF16) for i in range(2)]
        QKT16 = [sb(f"QKT16_{i}", [128, 512], BF16) for i in range(3)]
        QDT16 = [sb(f"QDT16_{i}", [128, 512], BF16) for i in range(3)]
        PB0 = [sb(f"PB0_{i}", [128, 512], BF16) for i in range(2)]
        PH0 = [sb(f"PH0_{i}", [128, 4, 128], BF16) for i in range(2)]
        PH1 = [sb(f"PH1_{i}", [128, 4, 256], BF16) for i in range(2)]
        YTP16 = [sb(f"YTP16_{i}", [128, 512], BF16) for i in range(3)]
        SMC = sb("SMC", [128, 32])
        PBX = [sb(f"PBX_{i}", [128, 512], BF16) for i in range(2)]
        PHX = [sb(f"PHX_{i}", [128, 4, 256], BF16) for i in range(2)]
        HF16 = sb("HF16", [128, 512], BF16)
        U32 = [sb(f"U32_{i}", [128, 512]) for i in range(2)]
        WT16 = [sb(f"WT16_{i}", [128, 512], BF16) for i in range(2)]
        VNZ = [sb(f"VNZ_{i}", [128, 512], BF16) for i in range(2)]
        S32 = sb("S32", [128, 512])
        S16 = sb("S16", [128, 512], BF16)
        ORAW = sb("ORAW", [128, 512])
        SQO = sb("SQO", [128, 512])
        YDN16 = sb("YDN16", [128, 512], BF16)
        YTD16 = sb("YTD16", [128, 512], BF16)

        def load(dst_ap, src_ap, key, wkeys):
            sem(key)
            S.dma("sp", dst_ap, src_ap, [], wkeys, key)

        for k in CONST_SHAPES:
            if k not in STAGED:
                load(C32[k][:], cd[k], "ld_c_" + k, ["C_" + k])
        load(NWC[:], nw_d, "ld_nw", ["NWC"])
        load(FNW[:], fnw_d.partition_broadcast(128), "ld_fnw", ["FNW"])
        load(DNWC[:], dnwc_d, "ld_dnwc", ["DNWC"])
        load(PSROW[:], psrow_d.partition_broadcast(128), "ld_psrow", ["PSROW"])
        load(ALOG[:], alog_d.partition_broadcast(128), "ld_alog", ["ALOG"])
        load(DTB[:], dtb_d.partition_broadcast(128), "ld_dtb", ["DTB"])
        load(CONVW[:], convw_d, "ld_convw", ["CONVW"])
        S.op("pool", lambda e: e.memset(EPSC[:], EPS), [], ["EPSC"])
        S.op("pool", lambda e: e.memset(ONEC[:], 1.0), [], ["ONEC"])
        S.op("pool", lambda e: e.memset(ONES128[:], 1.0), [], ["ONES128"])
        S.op("pool", lambda e: e.memset(NHALF[:], -0.5), [], ["NHALF"])
        cp("dve", ONES16[:], ONEC[:], ["ONEC"], ["ONES16"])
        cp("dve", ID16[:], C32["ident32"][:], ["C_ident32"], ["ID16"])
        act(NEGA[:], ALOG[:], AF.Exp, ["ALOG"], ["NEGA"])
        ts("dve", NEGA[:], NEGA[:], -1.0, None, ALU.mult, None, ["NEGA"], ["NEGA"])
        for ci in range(48):
            ts("dve", DIAG16[:, ci, :], C32["ident32"][:], CONVW[:, ci:ci + 1], None, ALU.mult, None,
               ["C_ident32", "CONVW"], [f"DIAG16_{ci}"])
        stage_slots = [(X32[0], "X32_0"), (X32[1], "X32_1"), (X32[2], "X32_2"), (H32, "H32"), (OUT32[0], "OUT32_0"), (OUT32[1], "OUT32_1")]
        dq = ["sp", "act"]
        dqi = [0]

        def dma_q():
            q_ = dq[dqi[0] % len(dq)]
            dqi[0] += 1
            return q_
        sidx = [0]

        def stage():
            s = stage_slots[sidx[0] % len(stage_slots)]
            sidx[0] += 1
            return s

        cv_eng = ["dve", "dve"]
        cvi = [0]

        def conv_eng():
            e = cv_eng[cvi[0] % 2]
            cvi[0] += 1
            return e

        STAGED_T = {"MS16": MS16, "MU16": MU16, "WCT16": WCT16, "WPT16": WPT16, "WC0HI16": WC0HI16, "WC0LO16": WC0LO16}
        for k, tname in STAGED.items():
            st, stk = stage()
            sem("ld_" + stk)
            S.dma("sp", st[:, 0:512], cd[k], [], [stk], "ld_" + stk)
            cp("dve", STAGED_T[tname][:], st[:, 0:512], [stk], [tname])
        st, stk = stage()
        sem("ld_" + stk)
        S.dma("sp", st[:, 0:512].rearrange("p (g d) -> p g d", g=4), poolw_d, [], [stk], "ld_" + stk)
        tt("dve", POOLW16[:].rearrange("p g d -> p (g d)"), st[:, 0:512], PSROW[:], ALU.mult, [stk, "PSROW"], ["POOLW16"])
        win_v = win_d.rearrange("(kc p) n -> p kc n", p=128)
        for kc in range(8):
            for (c0, cw) in ((0, 1024), (1024, 1024), (2048, 1024), (3072, 8)):
                st, stk = stage()
                sem("ld_" + stk)
                S.dma(dma_q(), st[:, 0:cw], win_v[:, kc, c0:c0 + cw], [], [stk], "ld_" + stk)
                ts("dve", WIN16[:, kc, c0:c0 + cw], st[:, 0:cw], NWC[:, kc:kc + 1], None, ALU.mult, None, [stk, "NWC"], ["WIN16"])
        wout_v = wout_d.rearrange("(kc p) n -> p kc n", p=128)
        for kc in range(8):
            st, stk = stage()
            sem("ld_" + stk)
            S.dma(dma_q(), st[:, :], wout_v[:, kc, :], [], [stk], "ld_" + stk)
            if kc < 4:
                cp(conv_eng(), WOUT16[:, kc, :], st[:, :], [stk], ["WOUT16"])
            else:
                ts("dve", WOUT16[:, kc, :], st[:, :], DNWC[:, 0:1], None, ALU.mult, None, [stk, "DNWC"], ["WOUT16"])
        S.op("pool", lambda e: e.memset(VNZ[0][:], 0.0), [], ["VNZ_0"])
        S.op("pool", lambda e: e.memset(VNZ[1][:], 0.0), [], ["VNZ_1"])
        S.op("pool", lambda e: e.memset(S32[:], 0.0), [], ["S32"])
        S.op("pool", lambda e: e.memset(S16[:], 0.0), [], ["S16"])
        S.op("pool", lambda e: e.memset(RAW16[1][:], 0.0), [], ["RAW16_1_0", "RAW16_1_1", "RAW16_1_2", "RAW16_1_h"])

        x_v = x_d.rearrange("(n p) d -> n p d", p=128)
        out_v = out_d.rearrange("(n p) d -> n p d", p=128)
        for k in ("ld_X32_0", "ld_X32_1", "ld_X32_2", "st_OUT32_0", "st_OUT32_1"):
            sem(k)

        def rsqrt_small(out, in_, reads, writes, scale_in=1.0):
            n_ = fsz(out)
            ts("dve", out, in_, float(scale_in), EPS, ALU.mult, ALU.add, reads, writes)
            tt("pool", out, out, NHALF[:, 0:n_], ALU.pow, list(writes) + ["NHALF"], writes)

        def load_x(t):
            b3 = t % 3
            S.dma("sp", X32[b3][:], x_v[t], [], [f"X32_{b3}"], f"ld_X32_{b3}")

        def front(t):
            stage_[0] = "F"
            b = t % 2
            X, xk = X32[t % 3], f"X32_{t % 3}"
            b3 = t % 3
            SM = SMF[b3]
            smk = lambda n: f"SMF{b3}_{n}"
            act(XN16[:], X[:], AF.Square, [xk], ["XN16", smk("ssx")], accum=SM[:, 0:1])
            rsqrt_small(SM[:, 1:2], SM[:, 0:1], [smk("ssx")], [smk("rx")], scale_in=1.0 / D)
            ts("dve", XN16[:], X[:], SM[:, 1:2], None, ALU.mult, None, [xk, smk("rx")], ["XN16"])
            P, pk = pb()
            for kc in range(8):
                tr(P[:, kc * 128:(kc + 1) * 128], XN16[:, kc * 128:(kc + 1) * 128], ["XN16"], [pk])
            cp("act", XT16[:].rearrange("p k t -> p (k t)"), P[:, :], [pk], ["XT16"])
            S.cut()
            R, rk_ = RAW16[b], f"RAW16_{b}"
            Rp, rpk = RAW16[1 - b], f"RAW16_{1 - b}"
            cp("pool", R[:, :, 0:3], Rp[:, :, 128:131], [rpk + "_0", rpk + "_1", rpk + "_2"], [rk_ + "_h"])
            for grp in range(3):
                P, pk = pf()
                for ci in range(4):
                    c0 = C_Q + (grp * 4 + ci) * 128
                    for kc in range(8):
                        mm(P[:, ci * 128:(ci + 1) * 128], WIN16[:, kc, c0:c0 + 128], XT16[:, kc, :], ["WIN16", "XT16"], [pk],
                           start=(kc == 0), stop=(kc == 7))
                    if ci % 2 == 1:
                        S.cut()
                cp("dve" if grp == 0 else "act", R[:, grp * 4:(grp + 1) * 4, 3:131], v4(P[:, :]), [pk], [rk_ + f"_{grp}"])
            P, pk = pf()
            for ci in range(4):
                c0 = C_PZ + ci * 128
                for kc in range(8):
                    mm(P[:, ci * 128:(ci + 1) * 128], WIN16[:, kc, c0:c0 + 128], XT16[:, kc, :], ["WIN16", "XT16"], [pk],
                       start=(kc == 0), stop=(kc == 7))
                if ci % 2 == 1:
                    S.cut()
            act(SZ16[:], P[:, :], AF.Silu, [pk], ["SZ16"])
            P, pk = pf()
            for kc in range(8):
                mm(P[:, 0:512], XT16[:, kc, :], WIN16[:, kc, C_PU:C_PU + 512], ["WIN16", "XT16"], [pk],
                   start=(kc == 0), stop=(kc == 7))
            S.cut()
            Uc, uk = U16[b], f"U16_{b}"
            Up, upk = U16[1 - b], f"U16_{1 - b}"
            cp("dve", Uc[:], P[:, :], [pk], [uk])
            P, pk = pf()
            for kc in range(8):
                mm(P[:, 0:512], XT16[:, kc, :], WIN16[:, kc, C_DZ:C_DZ + 512], ["WIN16", "XT16"], [pk],
                   start=(kc == 0), stop=(kc == 7))
            S.cut()
            act(GATE[b3][:], P[:, :], AF.Silu, [pk], [f"GATE_{b3}"])
            P, pk = pf()
            for kc in range(8):
                mm(P[:, 0:8], XT16[:, kc, :], WIN16[:, kc, C_BA:C_BA + 8], ["WIN16", "XT16"], [pk],
                   start=(kc == 0), stop=(kc == 7))
            cp("dve", SM[:, 8:16], P[:, 0:8], [pk], [smk("ba")])
            S.cut()
            for grp in range(3):
                P, pk = pf()
                for ci in range(4):
                    ch = grp * 4 + ci
                    for j in range(4):
                        mm(P[:, ci * 128:(ci + 1) * 128], DIAG16[:, ch * 4 + j, :], R[:, ch, j:j + 128], [f"DIAG16_{ch * 4 + j}", rk_ + "_h", rk_ + f"_{grp}"], [pk],
                           start=(j == 0), stop=(j == 3))
                act(QKV16[:, grp * 4:(grp + 1) * 4, :].rearrange("p c t -> p (c t)"), P[:, :], AF.Silu, [pk], [f"QKV16_{grp}"])
                S.cut()
            Pm, pkm = pf()
            for g in range(4):
                gs = slice(g * 128, (g + 1) * 128)
                if t == 0:
                    mm(Pm[:, gs], Uc[:, gs], WCT16[:, gs], [uk, "WCT16"], [pkm], start=True, stop=False)
                    mm(Pm[:, gs], Uc[:, gs], WC0HI16[:, gs], [uk, "WC0HI16"], [pkm], start=False, stop=False)
                    mm(Pm[:, gs], Uc[:, gs], WC0LO16[:, gs], [uk, "WC0LO16"], [pkm], start=False, stop=True)
                else:
                    mm(Pm[:, gs], Up[:, gs], WPT16[:, gs], [upk, "WPT16"], [pkm], start=True, stop=False)
                    mm(Pm[:, gs], Uc[:, gs], WCT16[:, gs], [uk, "WCT16"], [pkm], start=False, stop=True)
            cp("act", MIX16[:], Pm[:, :], [pkm], ["MIX16"])
            S.cut()
            Pp, pkp = pf()
            for g in range(4):
                gs = slice(g * 128, (g + 1) * 128)
                mm(Pp[:, gs], POOLW16[:, g, :], MIX16[:, gs], ["POOLW16", "MIX16"], [pkp])
            tt("dve", YTP16[b3][:], Pp[:, :], SZ16[:], ALU.mult, [pkp, "SZ16"], [f"YTP16_{b3}"])
            S.cut()
            tt("dve", SQ16[:], QKV16[:, 0:8, :], QKV16[:, 0:8, :], ALU.mult, ["QKV16_0", "QKV16_1"], ["SQ16"])
            P, pk = pf()
            for c in range(8):
                mm(P[:, c:c + 1], SQ16[:, c, :], ONES16[:, 0:1], ["SQ16", "ONES16"], [pk])
            cp("dve", SM[:, 16:24], P[:, 0:8], [pk], [smk("ssqk")])
            S.cut()
            rsqrt_small(SM[:, 24:32], SM[:, 16:24], [smk("ssqk")], [smk("rqk")])
            ts("dve", SM[:, 24:28], SM[:, 24:28], float(128 ** -0.5), None, ALU.mult, None, [smk("rqk")], [smk("rqk")])
            act(SM[:, 32:36], SM[:, 8:12], AF.Exp, [smk("ba")], [smk("beta")], scale=-1.0)
            ts("dve", SM[:, 32:36], SM[:, 32:36], 1.0, None, ALU.add, None, [smk("beta")], [smk("beta")])
            S.op("dve", lambda e: e.reciprocal(out=SM[:, 32:36], in_=SM[:, 32:36]), [smk("beta")], [smk("beta")])
            tt("dve", SM[:, 36:40], SM[:, 12:16], DTB[:, 0:4], ALU.add, [smk("ba"), "DTB"], [smk("g")])
            act(SM[:, 36:40], SM[:, 36:40], AF.Exp, [smk("g")], [smk("g")])
            act(SM[:, 36:40], SM[:, 36:40], AF.Ln, [smk("g"), "ONEC"], [smk("g")], bias=ONEC[:, 0:1])
            tt("dve", SM[:, 36:40], SM[:, 36:40], NEGA[:, 0:4], ALU.mult, [smk("g"), "NEGA"], [smk("g")])
            P, pk = pf()
            mm(P[:, 0:4], C32["u2"][:], SM[:, 36:40], ["C_u2", smk("g")], [pk])
            mm(P[:, 4:8], C32["cisame"][:], SM[:, 36:40], ["C_cisame", smk("g")], [pk])
            mm(P[:, 8:12], C32["ci0"][:], SM[:, 36:40], ["C_ci0", smk("g")], [pk])
            mm(P[:, 12:16], C32["ci1"][:], SM[:, 36:40], ["C_ci1", smk("g")], [pk])
            cp("dve", SM[:, 40:56], P[:, 0:16], [pk], [smk("gc"), smk("gc2")])
            tt("dve", SM[:, 44:48], SM[:, 44:48], SM[:, 40:44], ALU.subtract, [smk("gc")], [smk("gc2")])
            act(SM[:, 56:72], SM[:, 40:56], AF.Exp, [smk("gc"), smk("gc2")], [smk("ex")])
            RK = SM[:, 28:32]
            BETA = SM[:, 32:36]
            tt("dve", SM[:, 84:88], RK, BETA, ALU.mult, [smk("rqk"), smk("beta")], [smk("c0")])
            stt("dve", SM[:, 72:76], SM[:, 84:88], -1.0, RK, ALU.mult, ALU.mult, [smk("c0"), smk("rqk")], [smk("c1")])
            stt("dve", SM[:, 76:80], SM[:, 72:76], -1.0, SM[:, 56:60], ALU.mult, ALU.mult, [smk("c1"), smk("ex")], [smk("c2")])
            S.cut()
            Pk, pkk = pb()
            for h in range(4):
                tr(Pk[:, h * 128:(h + 1) * 128], QKV16[:, 4 + h, :], ["QKV16_1"], [pkk])
                tr(Pk[:, 512 + h * 128:512 + (h + 1) * 128], QKV16[:, 8 + h, :], ["QKV16_2"], [pkk])
            tt("dve", v4(KDEC16[b3][:]), v4(Pk[:, 0:512]), bc4(SM[:, 60:64]), ALU.mult, [pkk, smk("ex")], [f"KDEC16_{b3}"])
            tt("dve", v4(KBA16[b][:]), v4(Pk[:, 0:512]), bc4(SM[:, 76:80]), ALU.mult, [pkk, smk("c2")], [f"KBA16_{b}"])
            tt("dve", v4(VB16[b][:]), v4(Pk[:, 512:1024]), bc4(SM[:, 84:88]), ALU.mult, [pkk, smk("c0")], [f"VB16_{b}"])
            S.cut()
            tt("dve", v4(GU[:]), v4(C32["u2rep"][:]), bc4(SM[:, 36:40]), ALU.mult, ["C_u2rep", smk("g")], ["GU"])
            Pg, pkg = pf()
            mm(Pg[:, :], ONES128[:], GU[:], ["ONES128", "GU"], [pkg])
            ts("dve", SM[:, 88:92], SM[:, 40:44], -1.0, None, ALU.mult, None, [smk("gc")], [smk("ngc")])
            for h in range(4):
                act(DA[:, h * 128:(h + 1) * 128], Pg[:, h * 128:(h + 1) * 128], AF.Abs, [pkg, smk("ngc")], ["DA"],
                    bias=SM[:, 88 + h:89 + h])
            act(AROW[:], Pg[:, :], AF.Exp, [pkg], ["AROW"])
            act(EE[:], DA[:], AF.Exp, ["DA"], ["EE16"], scale=-1.0)
            S.cut()
            tt("dve", GT[:], EE[:], MU16[:], ALU.mult, ["EE16", "MU16"], ["GT"])
            tt("dve", GSB[:], EE[:], MS16[:], ALU.mult, ["EE16", "MS16"], ["GSB"])
            tt("dve", QDT16[b3][:], QKV16[:, 0:4, :].rearrange("p c t -> p (c t)"), AROW[:], ALU.mult, ["QKV16_0", "AROW"], [f"QDT16_{b3}"])
            Pkk, pkkk = pf()
            for h in range(4):
                mm(Pkk[:, h * 128:(h + 1) * 128], QKV16[:, 4 + h, :], QKV16[:, 4 + h, :], ["QKV16_1"], [pkkk])
            Pqk, pkqk = pf()
            for h in range(4):
                mm(Pqk[:, h * 128:(h + 1) * 128], QKV16[:, 4 + h, :], QKV16[:, h, :], ["QKV16_1", "QKV16_0"], [pkqk])
            S.cut()
            for h in range(4):
                hs = slice(h * 128, (h + 1) * 128)
                stt("dve", PB0[b][:, hs], Pkk[:, hs], SM[:, 72 + h:73 + h], GSB[:, hs], ALU.mult, ALU.mult,
                    [pkkk, "GSB", smk("c1")], [f"PB0_{b}"])
            tt("dve", QKT16[b3][:], Pqk[:, :], GT[:], ALU.mult, [pkqk, "GT"], [f"QKT16_{b3}"])
            P3, pk3 = pb()
            for h in range(4):
                tr(P3[:, h * 128:(h + 1) * 128], PB0[b][:, h * 128:(h + 1) * 128], [f"PB0_{b}"], [pk3])
            cp("act", PH0[b][:, :, :], v4(P3[:, 0:512]), [pk3], [f"PH0_{b}"])
            tt("dve", PH1[b][:, :, 128:256], v4(P3[:, 0:512]), ID16[:].unsqueeze(1).to_broadcast([128, 4, 128]), ALU.add,
               [pk3, "ID16"], [f"PH1_{b}_H"])
            S.cut()

        def stageB(t):
            stage_[0] = "B"
            b = t % 2
            X, xk = X32[t % 3], f"X32_{t % 3}"
            SM = SMF[b]
            smk = lambda n: f"SMF{b}_{n}"
            seq = [(PB0[b], f"PB0_{b}", PH0[b], f"PH0_{b}", None), (PBX[0], "PBX_0", PH1[b], f"PH1_{b}_P", f"PH1_{b}_H"),
                   (PBX[1], "PBX_1", PHX[1], "PHX_1_P", "PHX_1_H"), (PBX[0], "PBX_0", PHX[0], "PHX_0_P", "PHX_0_H"),
                   (PBX[1], "PBX_1", PHX[1], "PHX_1_P", "PHX_1_H"), (PBX[0], "PBX_0", PHX[0], "PHX_0_P", "PHX_0_H")]
            for k in range(5):
                PBc, kb, PHc, khP, khH = seq[k]
                PBn, kbn, PHn, khnP, khnH = seq[k + 1]
                Pa, pka = pf()
                for h in range(4):
                    mm(Pa[:, h * 128:(h + 1) * 128], PHc[:, h, 0:128], PBc[:, h * 128:(h + 1) * 128], [khP, kb], [pka])
                cp("act", PBn[:], Pa[:, :], [pka], [kbn])
                if k == 0:
                    Pb_, pkb = pf()
                    for h in range(4):
                        mm(Pb_[:, h * 128:(h + 1) * 128], PBc[:, h * 128:(h + 1) * 128], PHc[:, h, 0:128], [kb, khP], [pkb])
                    cp("dve", PHn[:, :, 0:128], v4(Pb_[:, :]), [pkb], [khnP])
                elif k < 4:
                    Pb0, pkb0 = pf()
                    Pb1, pkb1 = pf()
                    for h in range(4):
                        Pd = Pb0 if h < 2 else Pb1
                        pkd = pkb0 if h < 2 else pkb1
                        mm(Pd[:, (h % 2) * 256:(h % 2) * 256 + 256], PBc[:, h * 128:(h + 1) * 128], PHc[:, h, :], [kb, khP, khH], [pkd])
                    for half, (Pd, pkd) in enumerate(((Pb0, pkb0), (Pb1, pkb1))):
                        pv = Pd[:, :].rearrange("p (h c) -> p h c", h=2)
                        hs = slice(half * 2, half * 2 + 2)
                        cp("act", PHn[:, hs, 0:128], pv[:, :, 0:128], [pkd], [khnP])
                        tt("dve", PHn[:, hs, 128:256], pv[:, :, 128:256], PHc[:, hs, 128:256], ALU.add, [pkd, khH], [khnH])
                else:
                    Pb_, pkb = pf()
                    for h in range(4):
                        mm(Pb_[:, h * 128:(h + 1) * 128], PBc[:, h * 128:(h + 1) * 128], PHc[:, h, 128:256], [kb, khH], [pkb])
                    tt("dve", PHn[:, :, 128:256], v4(Pb_[:, :]), PHc[:, :, 128:256], ALU.add, [pkb, khH], [khnH])
                S.cut()
            PBc, kb, PHc, khP, khH = seq[5]
            Pb_, pkb = pf()
            for h in range(4):
                mm(Pb_[:, h * 128:(h + 1) * 128], PBc[:, h * 128:(h + 1) * 128], PHc[:, h, 128:256], [kb, khH], [pkb])
            tt("dve", v4(HF16[:]), v4(Pb_[:, :]), PHc[:, :, 128:256], ALU.add, [pkb, khH], ["HF16"])
            S.cut()
            Pu, pku = pf()
            for h in range(4):
                mm(Pu[:, h * 128:(h + 1) * 128], HF16[:, h * 128:(h + 1) * 128], VB16[b][:, h * 128:(h + 1) * 128], ["HF16", f"VB16_{b}"], [pku])
            cp("act", U32[b][:], Pu[:, :], [pku], [f"U32_{b}"])
            Pw, pkw = pf()
            for h in range(4):
                mm(Pw[:, h * 128:(h + 1) * 128], KBA16[b][:, h * 128:(h + 1) * 128], HF16[:, h * 128:(h + 1) * 128], [f"KBA16_{b}", "HF16"], [pkw])
            cp("dve", WT16[b][:], Pw[:, :], [pkw], [f"WT16_{b}"])
            S.cut()
        def stageC(t):
            stage_[0] = "C"
            b = t % 2
            b3 = t % 3
            X, xk = X32[t % 3], f"X32_{t % 3}"
            SM = SMF[b3]
            smk = lambda n: f"SMF{b3}_{n}"
            for c in range(2):
                rs = slice(64 * c, 64 * c + 64)
                VN, vnk = VNZ[c], f"VNZ_{c}"
                Pr, pkr = pf()
                for h in range(4):
                    hs = slice(h * 128, (h + 1) * 128)
                    mm(Pr[:, hs], WT16[b][:, hs], S16[:, hs], [f"WT16_{b}", "S16"], [pkr])
                tt("dve", VN[rs, :], U32[b][rs, :], Pr[rs, :], ALU.subtract, [f"U32_{b}", pkr], [vnk])
                S.cut()
                Po, pko = pf()
                for h in range(4):
                    hs = slice(h * 128, (h + 1) * 128)
                    mm(Po[:, hs], QDT16[b3][:, hs], S16[:, hs], [f"QDT16_{b3}", "S16"], [pko], start=True, stop=False)
                    mm(Po[:, hs], QKT16[b3][:, hs], VN[:, hs], [f"QKT16_{b3}", vnk], [pko], start=False, stop=True)
                cp("act", ORAW[rs, :], Po[rs, :], [pko], ["ORAW"])
                Ps, pks = pf()
                for h in range(4):
                    hs = slice(h * 128, (h + 1) * 128)
                    mm(Ps[:, hs], KDEC16[b3][:, hs], VN[:, hs], [f"KDEC16_{b3}", vnk], [pks])
                for h in range(4):
                    hs = slice(h * 128, (h + 1) * 128)
                    stt("dve", S32[:, hs], S32[:, hs], SM[:, 64 + 4 * c + h:65 + 4 * c + h], Ps[:, hs], ALU.mult, ALU.add,
                        ["S32", smk("ex"), pks], ["S32"])
                cp("act", S16[:], S32[:], ["S32"], ["S16"])
                S.cut()
            for h in range(4):
                act(SQO[:, h * 128:(h + 1) * 128], ORAW[:, h * 128:(h + 1) * 128], AF.Square, ["ORAW"], ["SQO", "SMC_sso"],
                    accum=SMC[:, h:h + 1])
            RQ = SM[:, 24:28]
            tt("dve", SMC[:, 4:8], RQ, RQ, ALU.mult, [smk("rqk")], ["SMC_o1"])
            stt("dve", SMC[:, 4:8], SMC[:, 4:8], 1.0 / 128.0, SMC[:, 0:4], ALU.mult, ALU.mult, ["SMC_o1", "SMC_sso"], ["SMC_o1"])
            rsqrt_small(SMC[:, 8:12], SMC[:, 4:8], ["SMC_o1"], ["SMC_o2"])
            tt("dve", SMC[:, 8:12], SMC[:, 8:12], RQ, ALU.mult, ["SMC_o2", smk("rqk")], ["SMC_o2"])
            for h in range(4):
                hs = slice(h * 128, (h + 1) * 128)
                stt("dve", YDN16[:, hs], ORAW[:, hs], SMC[:, 8 + h:9 + h], GATE[b3][:, hs], ALU.mult, ALU.mult,
                    ["ORAW", "SMC_o2", f"GATE_{b3}"], ["YDN16"])
            S.cut()
            P4, pk4 = pb()
            for h in range(4):
                tr(P4[:, h * 128:(h + 1) * 128], YDN16[:, h * 128:(h + 1) * 128], ["YDN16"], [pk4])
            cp("act", YTD16[:], P4[:, 0:512], [pk4], ["YTD16"])
            S.cut()
            for nh in range(2):
                Pq, pkq = pf()
                for e8 in range(8):
                    lhs = YTP16[b3][:, e8 * 128:(e8 + 1) * 128] if e8 < 4 else YTD16[:, (e8 - 4) * 128:(e8 - 3) * 128]
                    mm(Pq[:, 0:512], lhs, WOUT16[:, e8, nh * 512:(nh + 1) * 512],
                       [f"YTP16_{b3}" if e8 < 4 else "YTD16", "WOUT16"], [pkq], start=(e8 == 0), stop=(e8 == 7))
                tt("dve", H32[:, nh * 512:(nh + 1) * 512], Pq[:, :], X[:, nh * 512:(nh + 1) * 512], ALU.add, [pkq, xk], ["H32"])
            if t + 3 < n_tiles:
                load_x(t + 3)
            act(SQO[:].bitcast(BF16), H32[:], AF.Square, ["H32"], ["SQO", "SMC_ssh"], accum=SMC[:, 12:13])
            rsqrt_small(SMC[:, 13:14], SMC[:, 12:13], ["SMC_ssh"], ["SMC_rh"], scale_in=1.0 / D)
            O, ok = OUT32[b], f"OUT32_{b}"
            stt(cfg.get("out_eng", "dve"), O[:], H32[:], SMC[:, 13:14], FNW[:], ALU.mult, ALU.mult, ["H32", "SMC_rh", "FNW"], [ok])
            S.dma("sp", out_v[t], O[:], [ok], [], f"st_OUT32_{b}")
            S.cut()

        for t0 in range(min(3, n_tiles)):
            load_x(t0)
        for t in range(n_tiles):
            front(t)
            stageB(t)
            stageC(t)
        if cfg.get("junk"):
            g_, f_, m_ = cfg["junk"]

            jb_ = cfg.get("junk_bank", 7)

            def _mk_junk():
                o_ = _Op("pe", lambda e: e.matmul(PSF[jb_][:, 0:128], lhsT=ID16[:], rhs=ID16[:], start=True, stop=True,
                                                  skip_group_check=True), ["ID16"], [])
                return o_
            Sched.JUNK = (g_, f_, m_, _mk_junk)
        else:
            Sched.JUNK = None
        Sched.ACT_PEN = float(cfg.get("act_pen", 1400.0))
        Sched.JITTER = tuple(cfg["jitter"]) if cfg.get("jitter") else None
        Sched.ATTACH_WAIT = bool(cfg.get("attach_wait", True))
        Sched.XLAT = float(cfg.get("xlat", 700.0))
        S.schedule()
        print("[sched] est_total_us=%.1f n_ops=%d" % (S.est_total / 1e3, len(S.ops)))
        S.emit(sems)
    return nc, list(dbg_out.keys())


def _core_inputs(inputs, b, consts):
    f = lambda a: np.ascontiguousarray(np.asarray(a, dtype=np.float32))
    conv_w = np.asarray(inputs["conv_w"][0], np.float32)
    m = {
        "x": f(inputs["x"][b]),
        "w_in": f(inputs["w_in"][0]),
        "w_out": f(inputs["w_out"][0]),
        "pool_w": f(np.transpose(np.asarray(inputs["pool_w"][0]), (1, 0, 2))),
        "pool_scale": f(np.asarray(inputs["pool_scale"][0]).reshape(512)),
        "conv_w": f(np.transpose(conv_w.reshape(4, 12, 128), (2, 1, 0)).reshape(128, 48)),
        "a_log": f(inputs["a_log"][0]),
        "dt_bias": f(inputs["dt_bias"][0]),
        "dn_norm_w": f(np.asarray(inputs["dn_norm_w"][0]).reshape(128, 1)),
        "norm_w": f(np.asarray(inputs["norm_w"][0]).reshape(8, 128).T),
        "final_norm_w": f(inputs["final_norm_w"]),
    }
    for k, v in consts.items():
        m["c_" + k] = f(v)
    return m


def kernel(**inputs):
    consts = _consts()
    nc, _ = build_nc()
    in_maps = [_core_inputs(inputs, b, consts) for b in range(8)]
    res = run_bass_kernel_spmd(nc, in_maps, core_ids=list(range(8)))
    out = np.stack([np.asarray(r["out"], dtype=np.float32) for r in res.results], axis=0)
    return out
```

```python
import contextlib
import numpy as np
import ml_dtypes
import concourse.bass as bass
import concourse.mybir as mybir
from concourse.bass_utils import run_bass_kernel_spmd

F32 = mybir.dt.float32
BF16 = mybir.dt.bfloat16
AF = mybir.ActivationFunctionType
ALU = mybir.AluOpType
AX = mybir.AxisListType

SEQ = 4096
D = 1024
DIN = 3080
NT = SEQ // 128
EPS = 1e-6
POOL_WINDOWS = (2, 4, 8, 16)
C_PU, C_PZ, C_Q, C_K, C_V, C_DZ, C_BA = 0, 512, 1024, 1536, 2048, 2560, 3072


class _Op:
    __slots__ = ("eng", "fn", "reads", "writes", "dma_key", "idx", "deps", "sig", "dma_val", "waits", "cost", "tset")

    def __init__(self, eng, fn, reads, writes, dma_key=None):
        self.eng = eng
        self.fn = fn
        self.reads = tuple(reads)
        self.writes = tuple(writes)
        self.dma_key = dma_key
        self.sig = None
        self.dma_val = None
        self.cost = 100.0
        self.tset = None


class Sched:
    COMPUTE = ("pe", "act", "dve", "pool")
    XLAT = 250.0
    NO_WAR = 0
    ACT_PEN = 1400.0
    JITTER = None
    ATTACH_WAIT = False
    JUNK = None
    PS_BIAS = 0.0

    def __init__(self, nc):
        self.nc = nc
        self.ops = []
        self.cuts = []

    def cut(self):
        self.cuts.append(len(self.ops))

    def op(self, eng, fn, reads=(), writes=()):
        extra = [r for r in reads if r.startswith("ps") and r not in writes]
        if extra:
            writes = list(writes) + extra
        o = _Op(eng, fn, reads, writes)
        self.ops.append(o)
        return o

    def dma(self, queue, out, in_, reads, writes, key):
        o = _Op(queue, lambda e: e.dma_start(out=out, in_=in_), reads, writes, dma_key=key)
        nb = 4
        for d_ in out.shape:
            nb *= int(d_)
        o.cost = 2200.0 + nb / 200.0
        self.ops.append(o)
        return o

    def schedule(self):
        ops = self.ops
        n = len(ops)
        last_w, readers = {}, {}
        preds = [set() for _ in range(n)]
        for i, o in enumerate(ops):
            for r in o.reads:
                w = last_w.get(r)
                if w is not None:
                    preds[i].add(w)
            for wk in o.writes:
                w = last_w.get(wk)
                if w is not None:
                    preds[i].add(w)
                if not (Sched.NO_WAR and (Sched.NO_WAR == 2 or not wk.startswith("ps"))):
                    for rd in readers.get(wk, ()):
                        preds[i].add(rd)
            preds[i].discard(i)
            for r in o.reads:
                readers.setdefault(r, set()).add(i)
            for wk in o.writes:
                last_w[wk] = i
                readers[wk] = set()
        succs = [[] for _ in range(n)]
        npred = [len(p) for p in preds]
        for i, p in enumerate(preds):
            for j in p:
                succs[j].append(i)
        free = {}
        rtime = [0.0] * n
        ready = set(i for i in range(n) if npred[i] == 0)
        order = []
        self.trace = []
        jit = None
        if Sched.JITTER is not None:
            import random as _random
            rng_ = _random.Random(Sched.JITTER[0])
            jit = [rng_.uniform(0.0, Sched.JITTER[1]) for _ in range(n)]
        open_ps = {}
        self.bind = {}
        self.seq_ops = list(ops)
        act_set = None
        XLAT = Sched.XLAT
        while ready:
            best, bk = None, None
            for i in ready:
                o = ops[i]
                st = max(free.get(o.eng, 0.0), rtime[i])
                if o.eng == "act" and o.tset is not None and o.tset != act_set:
                    st += Sched.ACT_PEN
                pr = st + (jit[i] if jit is not None else 0.0)
                if Sched.PS_BIAS:
                    for kk_ in o.writes:
                        if kk_.startswith("ps") and open_ps.get(kk_, 0) > 0:
                            pr = st - Sched.PS_BIAS
                            break
                k = (pr, i, st)
                if bk is None or k < bk:
                    best, bk = i, k
            i = best
            o = ops[i]
            ready.discard(i)
            st = bk[2]
            if o.eng == "pe" and Sched.JUNK is not None:
                gap_ = st - free.get("pe", 0.0)
                if gap_ > Sched.JUNK[0] and free.get("pe", 0.0) > 0.0:
                    for _ in range(min(int(gap_ * Sched.JUNK[1] / 60.0), Sched.JUNK[2])):
                        order.append(Sched.JUNK[3]())
            for kk_ in o.writes:
                if kk_.startswith("ps"):
                    if o.eng == "pe":
                        open_ps[kk_] = open_ps.get(kk_, 0) + 1
                    else:
                        open_ps[kk_] = 0
            if o.eng == "act" and o.tset is not None:
                act_set = o.tset
            if o.dma_key is not None:
                free[o.eng] = st + 60.0
                fin = st + o.cost
            else:
                fin = st + o.cost
                free[o.eng] = fin
            order.append(o)
            self.trace.append((i, o.eng, st, fin, rtime[i], o.cost))
            for j in succs[i]:
                lat = XLAT if (ops[j].eng != o.eng or o.dma_key is not None) else 60.0
                if fin + lat > rtime[j]:
                    rtime[j] = fin + lat
                    self.bind[j] = i
                npred[j] -= 1
                if npred[j] == 0:
                    ready.add(j)
        self.ops = order
        self.est_total = max(free.values()) if free else 0.0

    def emit(self, sems):
        nc = self.nc
        last_w = {}
        readers = {}
        dma_cnt = {}
        for i, o in enumerate(self.ops):
            o.idx = i
            deps = set()
            for r in o.reads:
                w = last_w.get(r)
                if w is not None:
                    deps.add(w)
            for wk in o.writes:
                w = last_w.get(wk)
                if w is not None:
                    deps.add(w)
                for rd in readers.get(wk, ()):
                    deps.add(rd)
            deps.discard(o)
            o.deps = deps
            for r in o.reads:
                readers.setdefault(r, set()).add(o)
            for wk in o.writes:
                last_w[wk] = o
                readers[wk] = set()
            if o.dma_key is not None:
                dma_cnt[o.dma_key] = dma_cnt.get(o.dma_key, 0) + 1
                o.dma_val = 16 * dma_cnt[o.dma_key]
        needed = set()
        for o in self.ops:
            best = {}
            dma_deps = []
            for d in o.deps:
                if d.dma_key is not None:
                    dma_deps.append(d)
                    continue
                if d.eng == "pe" and o.eng == "pe":
                    continue
                b = best.get(d.eng)
                if b is None or d.idx > b.idx:
                    best[d.eng] = d
            o.deps = list(best.values()) + dma_deps
            for d in best.values():
                needed.add(d)
        cnt = {e: 0 for e in self.COMPUTE}
        for o in self.ops:
            if o in needed:
                cnt[o.eng] += 1
                o.sig = cnt[o.eng]
        waited = {}
        for o in self.ops:
            w = []
            for d in o.deps:
                if d.dma_key is not None:
                    k = ("dma", d.dma_key)
                    v = d.dma_val
                    s = sems[d.dma_key]
                else:
                    k = d.eng
                    v = d.sig
                    s = sems[d.eng]
                cur = waited.setdefault(o.eng, {}).get(k, 0)
                if cur >= v:
                    continue
                waited[o.eng][k] = v
                w.append((s, v))
            o.waits = w
        final_dma = {k: 16 * c for k, c in dma_cnt.items()}
        self.sig_counts = cnt
        by_eng = {}
        for o in self.ops:
            by_eng.setdefault(o.eng, []).append(o)
        engmap = {"pe": "tensor", "act": "scalar", "dve": "vector", "pool": "gpsimd", "sp": "sync"}

        def make(engname, ops):
            def body(eng):
                for o in ops:
                    wl = o.waits
                    attach = None
                    if Sched.ATTACH_WAIT and wl and o.dma_key is None:
                        attach = wl[-1]
                        wl = wl[:-1]
                    for (s, v) in wl:
                        eng.wait_ge(s, v)
                    ins = o.fn(eng)
                    if attach is not None:
                        ins._wait_ge(attach[0], attach[1])
                    if o.dma_key is not None:
                        ins.then_inc(sems[o.dma_key], 16)
                    elif o.sig is not None:
                        ins.then_inc(sems[o.eng], 1)
                if engname == "sp":
                    for k, v in final_dma.items():
                        eng.wait_ge(sems[k], v)
            return body

        with nc.Block() as block:
            for engname in ("sp", "pe", "act", "dve", "pool"):
                ops = by_eng.get(engname, [])
                if not ops and engname != "sp":
                    continue
                getattr(block, engmap[engname])(make(engname, ops))


def _consts():
    c = {}
    idx = np.arange(128)
    same = (idx[:, None] // 64) == (idx[None, :] // 64)
    c["ident32"] = np.eye(128, dtype=np.float32)
    c["u2"] = (same & (idx[:, None] <= idx[None, :])).astype(np.float32)
    c["u2rep"] = np.tile(c["u2"], (1, 4))
    c["cisame"] = same.astype(np.float32)
    c["ci0"] = np.repeat((idx < 64).astype(np.float32)[:, None], 128, axis=1)
    c["ci1"] = np.repeat((idx >= 64).astype(np.float32)[:, None], 128, axis=1)
    ms = (same & (idx[None, :] < idx[:, None])).astype(np.float32)
    mu = (same & (idx[None, :] >= idx[:, None])).astype(np.float32)
    c["ms"] = np.tile(ms, (1, 4))
    c["mu"] = np.tile(mu, (1, 4))
    c["ones32"] = np.ones((128, 1), np.float32)
    wct = np.zeros((128, 4, 128), np.float32)
    wpt = np.zeros((128, 4, 128), np.float32)
    wc0 = np.zeros((128, 4, 128), np.float32)
    for g, w in enumerate(POOL_WINDOWS):
        for t in range(128):
            for j in range(t - w + 1, t + 1):
                if j >= 0:
                    wct[j, g, t] += 1.0 / w
                else:
                    wpt[128 + j, g, t] += 1.0 / w
            wct[t, g, t] -= 1.0
            cnt = min(t + 1, w)
            for j in range(max(0, t - w + 1), t + 1):
                wc0[j, g, t] += 1.0 / cnt
            wc0[t, g, t] -= 1.0
    corr = wc0 - wct
    hi = corr.astype(ml_dtypes.bfloat16).astype(np.float32)
    lo = (corr - hi).astype(ml_dtypes.bfloat16).astype(np.float32)
    c["wct"] = wct.reshape(128, 512)
    c["wpt"] = wpt.reshape(128, 512)
    c["wc0hi"] = hi.reshape(128, 512)
    c["wc0lo"] = lo.reshape(128, 512)
    return c


CONST_SHAPES = {"ident32": [128, 128], "u2": [128, 128], "u2rep": [128, 512], "cisame": [128, 128], "ci0": [128, 128],
                "ci1": [128, 128], "ms": [128, 512], "mu": [128, 512], "ones32": [128, 1],
                "wct": [128, 512], "wpt": [128, 512], "wc0hi": [128, 512], "wc0lo": [128, 512]}


def build_nc(n_tiles=NT, dbg_tile=None, cfg=None):
    cfg = cfg or {}
    nc = bass.Bass("TRN2", target_bir_lowering=False)
    dram = {}

    def din(name, shape):
        dram[name] = nc.dram_tensor(name, list(shape), F32, kind="ExternalInput").ap()
        return dram[name]

    x_d = din("x", [SEQ, D])
    win_d = din("w_in", [D, DIN])
    wout_d = din("w_out", [D, D])
    poolw_d = din("pool_w", [128, 4, 128])
    psrow_d = din("pool_scale", [512])
    convw_d = din("conv_w", [128, 48])
    alog_d = din("a_log", [4])
    dtb_d = din("dt_bias", [4])
    dnwc_d = din("dn_norm_w", [128, 1])
    nw_d = din("norm_w", [128, 8])
    fnw_d = din("final_norm_w", [D])
    cd = {k: din("c_" + k, s) for k, s in CONST_SHAPES.items()}
    out_d = nc.dram_tensor("out", [SEQ, D], F32, kind="ExternalOutput").ap()
    dbg_out = {}

    S = Sched(nc)
    with contextlib.ExitStack() as es:
        def sb(name, shape, dt=F32):
            return es.enter_context(nc.sbuf_tensor(name, list(shape), dt))

        def psum(name, shape, dt):
            return es.enter_context(nc.psum_tensor(name, list(shape), dt))

        sems = {}

        def sem(k):
            if k not in sems:
                sems[k] = es.enter_context(nc.semaphore("s_" + k))
            return k

        for e in ("pe", "act", "dve", "pool"):
            sem(e)

        def act(out, in_, func, reads, writes, bias=None, scale=None, accum=None):
            kw = {}
            if bias is not None:
                kw["bias"] = bias
            if scale is not None:
                kw["scale"] = scale
            if accum is not None:
                kw["accum_out"] = accum
            o_ = S.op("act", lambda e: e.activation(out=out, in_=in_, func=func, **kw), reads, writes)
            o_.cost = 190.0 + 0.83 * fsz(out)
            o_.tset = {AF.Silu: "silu", AF.Exp: "lnexp", AF.Ln: "lnexp"}.get(func)
            return o_

        def fsz(ap):
            n_ = 1
            for d_ in ap.shape[1:]:
                n_ *= int(d_)
            return n_

        def ecost(eng, out):
            n_ = fsz(out)
            if eng == "act":
                return 190.0 + 0.83 * n_
            if eng == "pool":
                return 120.0 + 2.2 * n_
            return 110.0 + 0.8 * n_

        def cp(eng, out, in_, reads, writes):
            if eng == "act":
                o_ = S.op("act", lambda e: e.copy(out=out, in_=in_), reads, writes)
            else:
                o_ = S.op(eng, lambda e: e.tensor_copy(out=out, in_=in_), reads, writes)
            o_.cost = ecost(eng, out)
            return o_

        def tt(eng, out, in0, in1, op, reads, writes):
            o_ = S.op(eng, lambda e: e.tensor_tensor(out=out, in0=in0, in1=in1, op=op), reads, writes)
            o_.cost = ecost(eng, out) + (800.0 if op == ALU.pow else 0.0)
            return o_

        def stt(eng, out, in0, scalar, in1, op0, op1, reads, writes):
            o_ = S.op(eng, lambda e: e.scalar_tensor_tensor(out=out, in0=in0, scalar=scalar, in1=in1,
                                                            op0=op0, op1=op1), reads, writes)
            o_.cost = ecost(eng, out) * 1.1
            return o_

        def ts(eng, out, in0, s1, s2, op0, op1, reads, writes):
            if s2 is None:
                o_ = S.op(eng, lambda e: e.tensor_scalar(out=out, in0=in0, scalar1=s1, scalar2=None, op0=op0),
                          reads, writes)
            else:
                o_ = S.op(eng, lambda e: e.tensor_scalar(out=out, in0=in0, scalar1=s1, scalar2=s2, op0=op0, op1=op1),
                          reads, writes)
            o_.cost = ecost(eng, out)
            return o_

        def mm(out, lhsT, rhs, reads, writes, start=True, stop=True, skip=False):
            o_ = S.op("pe", lambda e: e.matmul(out, lhsT=lhsT, rhs=rhs, start=start, stop=stop,
                                               skip_group_check=skip), reads, writes)
            n_ = fsz(out)
            o_.cost = max(n_, 128) * 0.62 + 5.0
            if int(out.shape[0]) < 128:
                o_.cost = max(o_.cost, 160.0)
            if lhsT.dtype == F32:
                o_.cost = 4.0 * max(n_, 64) / 1.8 + 90.0
            return o_

        def tr(out, in_, reads, writes):
            o_ = S.op("pe", lambda e: e.transpose(out=out, in_=in_, identity=ID16[:]), list(reads) + ["ID16"], writes)
            o_.cost = 82.0
            return o_

        def bc4(ap4):
            return ap4.unsqueeze(2).to_broadcast([128, 4, 128])

        def v4(ap512):
            return ap512.rearrange("p (h d) -> p h d", h=4)

        PSF = [psum(f"psf{i}", [128, 512], F32) for i in range(8)]
        pools = cfg.get("pools", {"F": [0, 1, 2, 3], "B": [4, 5], "C": [6, 7]})
        pcnt = {"F": 0, "B": 0, "C": 0}
        stage_ = ["F"]

        def _next_bank():
            ids = pools[stage_[0]]
            i = ids[pcnt[stage_[0]] % len(ids)]
            pcnt[stage_[0]] += 1
            key = f"psf{i}"
            if cfg.get("inf_psum"):
                key = f"psf{i}_u{pcnt[stage_[0]]}{stage_[0]}"
            return i, key

        def pf():
            i, key = _next_bank()
            return PSF[i], key

        def pb():
            i, key = _next_bank()
            return PSF[i][:, :].bitcast(BF16), key

        WIN16 = sb("WIN16", [128, 8, DIN], BF16)
        WOUT16 = sb("WOUT16", [128, 8, D], BF16)
        POOLW16 = sb("POOLW16", [128, 4, 128], BF16)
        DIAG16 = sb("DIAG16", [128, 48, 128], BF16)
        NWC = sb("NWC", [128, 8])
        FNW = sb("FNW", [128, D])
        DNWC = sb("DNWC", [128, 1])
        PSROW = sb("PSROW", [128, 512])
        CONVW = sb("CONVW", [128, 48])
        ALOG = sb("ALOG", [128, 4])
        DTB = sb("DTB", [128, 4])
        NEGA = sb("NEGA", [128, 4])
        EPSC = sb("EPSC", [128, 1])
        ONEC = sb("ONEC", [128, 1])
        ONES16 = sb("ONES16", [128, 1], BF16)
        ONES128 = sb("ONES128", [128, 128])
        STAGED = {"ms": "MS16", "mu": "MU16", "wct": "WCT16", "wpt": "WPT16", "wc0hi": "WC0HI16", "wc0lo": "WC0LO16"}
        C32 = {k: sb("C_" + k, s) for k, s in CONST_SHAPES.items() if k not in STAGED}
        ID16 = sb("ID16", [128, 128], BF16)
        MS16 = sb("MS16", [128, 512], BF16)
        MU16 = sb("MU16", [128, 512], BF16)
        WCT16 = sb("WCT16", [128, 512], BF16)
        WPT16 = sb("WPT16", [128, 512], BF16)
        WC0HI16 = sb("WC0HI16", [128, 512], BF16)
        WC0LO16 = sb("WC0LO16", [128, 512], BF16)

        X32 = [sb(f"X32_{i}", [128, D]) for i in range(3)]
        H32 = sb("H32", [128, D])
        OUT32 = [sb(f"OUT32_{i}", [128, D]) for i in range(2)]
        XN16 = sb("XN16", [128, D], BF16)
        XT16 = sb("XT16", [128, 8, 128], BF16)
        RAW16 = [sb(f"RAW16_{i}", [128, 12, 131], BF16) for i in range(2)]
        QKV16 = sb("QKV16", [128, 12, 128], BF16)
        SQ16 = sb("SQ16", [128, 8, 128], BF16)
        SZ16 = sb("SZ16", [128, 512], BF16)
        U16 = [sb(f"U16_{i}", [128, 512], BF16) for i in range(2)]
        MIX16 = sb("MIX16", [128, 512], BF16)
        NHALF = sb("NHALF", [128, 8])
        GU = sb("GU", [128, 512])
        DA = sb("DA", [128, 512])
        EE = sb("EE16", [128, 512], BF16)
        AROW = sb("AROW", [128, 512], BF16)
        GSB = sb("GSB", [128, 512], BF16)
        GT = sb("GT", [128, 512], BF16)
        SMF = [sb(f"SMF_{i}", [128, 128]) for i in range(3)]
        GATE = [sb(f"GATE_{i}", [128, 512], BF16) for i in range(3)]
        KBA16 = [sb(f"KBA16_{i}", [128, 512], BF16) for i in range(2)]
        KDEC16 = [sb(f"KDEC16_{i}", [128, 512], BF16) for i in range(3)]
        VB16 = [sb(f"VB16_{i}", [128, 512], BF16) for i in range(2)]
        QKT16 = [sb(f"QKT16_{i}", [128, 512], BF16) for i in range(3)]
        QDT16 = [sb(f"QDT16_{i}", [128, 512], BF16) for i in range(3)]
        PB0 = [sb(f"PB0_{i}", [128, 512], BF16) for i in range(2)]
        PH0 = [sb(f"PH0_{i}", [128, 4, 128], BF16) for i in range(2)]
        PH1 = [sb(f"PH1_{i}", [128, 4, 256], BF16) for i in range(2)]
        YTP16 = [sb(f"YTP16_{i}", [128, 512], BF16) for i in range(3)]
        SMC = sb("SMC", [128, 32])
        PBX = [sb(f"PBX_{i}", [128, 512], BF16) for i in range(2)]
        PHX = [sb(f"PHX_{i}", [128, 4, 256], BF16) for i in range(2)]
        HF16 = sb("HF16", [128, 512], BF16)
        U32 = [sb(f"U32_{i}", [128, 512]) for i in range(2)]
        WT16 = [sb(f"WT16_{i}", [128, 512], BF16) for i in range(2)]
        VNZ = [sb(f"VNZ_{i}", [128, 512], BF16) for i in range(2)]
        S32 = sb("S32", [128, 512])
        S16 = sb("S16", [128, 512], BF16)
        ORAW = sb("ORAW", [128, 512])
        SQO = sb("SQO", [128, 512])
        YDN16 = sb("YDN16", [128, 512], BF16)
        YTD16 = sb("YTD16", [128, 512], BF16)

        def load(dst_ap, src_ap, key, wkeys):
            sem(key)
            S.dma("sp", dst_ap, src_ap, [], wkeys, key)

        for k in CONST_SHAPES:
            if k not in STAGED:
                load(C32[k][:], cd[k], "ld_c_" + k, ["C_" + k])
        load(NWC[:], nw_d, "ld_nw", ["NWC"])
        load(FNW[:], fnw_d.partition_broadcast(128), "ld_fnw", ["FNW"])
        load(DNWC[:], dnwc_d, "ld_dnwc", ["DNWC"])
        load(PSROW[:], psrow_d.partition_broadcast(128), "ld_psrow", ["PSROW"])
        load(ALOG[:], alog_d.partition_broadcast(128), "ld_alog", ["ALOG"])
        load(DTB[:], dtb_d.partition_broadcast(128), "ld_dtb", ["DTB"])
        load(CONVW[:], convw_d, "ld_convw", ["CONVW"])
        S.op("pool", lambda e: e.memset(EPSC[:], EPS), [], ["EPSC"])
        S.op("pool", lambda e: e.memset(ONEC[:], 1.0), [], ["ONEC"])
        S.op("pool", lambda e: e.memset(ONES128[:], 1.0), [], ["ONES128"])
        S.op("pool", lambda e: e.memset(NHALF[:], -0.5), [], ["NHALF"])
        cp("dve", ONES16[:], ONEC[:], ["ONEC"], ["ONES16"])
        cp("dve", ID16[:], C32["ident32"][:], ["C_ident32"], ["ID16"])
        act(NEGA[:], ALOG[:], AF.Exp, ["ALOG"], ["NEGA"])
        ts("dve", NEGA[:], NEGA[:], -1.0, None, ALU.mult, None, ["NEGA"], ["NEGA"])
        for ci in range(48):
            ts("dve", DIAG16[:, ci, :], C32["ident32"][:], CONVW[:, ci:ci + 1], None, ALU.mult, None,
               ["C_ident32", "CONVW"], [f"DIAG16_{ci}"])
        stage_slots = [(X32[0], "X32_0"), (X32[1], "X32_1"), (X32[2], "X32_2"), (H32, "H32"), (OUT32[0], "OUT32_0"), (OUT32[1], "OUT32_1")]
        dq = ["sp", "act"]
        dqi = [0]

        def dma_q():
            q_ = dq[dqi[0] % len(dq)]
            dqi[0] += 1
            return q_
        sidx = [0]

        def stage():
            s = stage_slots[sidx[0] % len(stage_slots)]
            sidx[0] += 1
            return s

        cv_eng = ["dve", "dve"]
        cvi = [0]

        def conv_eng():
            e = cv_eng[cvi[0] % 2]
            cvi[0] += 1
            return e

        STAGED_T = {"MS16": MS16, "MU16": MU16, "WCT16": WCT16, "WPT16": WPT16, "WC0HI16": WC0HI16, "WC0LO16": WC0LO16}
        for k, tname in STAGED.items():
            st, stk = stage()
            sem("ld_" + stk)
            S.dma("sp", st[:, 0:512], cd[k], [], [stk], "ld_" + stk)
            cp("dve", STAGED_T[tname][:], st[:, 0:512], [stk], [tname])
        st, stk = stage()
        sem("ld_" + stk)
        S.dma("sp", st[:, 0:512].rearrange("p (g d) -> p g d", g=4), poolw_d, [], [stk], "ld_" + stk)
        tt("dve", POOLW16[:].rearrange("p g d -> p (g d)"), st[:, 0:512], PSROW[:], ALU.mult, [stk, "PSROW"], ["POOLW16"])
        win_v = win_d.rearrange("(kc p) n -> p kc n", p=128)
        for kc in range(8):
            for (c0, cw) in ((0, 1024), (1024, 1024), (2048, 1024), (3072, 8)):
                st, stk = stage()
                sem("ld_" + stk)
                S.dma(dma_q(), st[:, 0:cw], win_v[:, kc, c0:c0 + cw], [], [stk], "ld_" + stk)
                ts("dve", WIN16[:, kc, c0:c0 + cw], st[:, 0:cw], NWC[:, kc:kc + 1], None, ALU.mult, None, [stk, "NWC"], ["WIN16"])
        wout_v = wout_d.rearrange("(kc p) n -> p kc n", p=128)
        for kc in range(8):
            st, stk = stage()
            sem("ld_" + stk)
            S.dma(dma_q(), st[:, :], wout_v[:, kc, :], [], [stk], "ld_" + stk)
            if kc < 4:
                cp(conv_eng(), WOUT16[:, kc, :], st[:, :], [stk], ["WOUT16"])
            else:
                ts("dve", WOUT16[:, kc, :], st[:, :], DNWC[:, 0:1], None, ALU.mult, None, [stk, "DNWC"], ["WOUT16"])
        S.op("pool", lambda e: e.memset(VNZ[0][:], 0.0), [], ["VNZ_0"])
        S.op("pool", lambda e: e.memset(VNZ[1][:], 0.0), [], ["VNZ_1"])
        S.op("pool", lambda e: e.memset(S32[:], 0.0), [], ["S32"])
        S.op("pool", lambda e: e.memset(S16[:], 0.0), [], ["S16"])
        S.op("pool", lambda e: e.memset(RAW16[1][:], 0.0), [], ["RAW16_1_0", "RAW16_1_1", "RAW16_1_2", "RAW16_1_h"])

        x_v = x_d.rearrange("(n p) d -> n p d", p=128)
        out_v = out_d.rearrange("(n p) d -> n p d", p=128)
        for k in ("ld_X32_0", "ld_X32_1", "ld_X32_2", "st_OUT32_0", "st_OUT32_1"):
            sem(k)

        def rsqrt_small(out, in_, reads, writes, scale_in=1.0):
            n_ = fsz(out)
            ts("dve", out, in_, float(scale_in), EPS, ALU.mult, ALU.add, reads, writes)
            tt("pool", out, out, NHALF[:, 0:n_], ALU.pow, list(writes) + ["NHALF"], writes)

        def load_x(t):
            b3 = t % 3
            S.dma("sp", X32[b3][:], x_v[t], [], [f"X32_{b3}"], f"ld_X32_{b3}")

        def front(t):
            stage_[0] = "F"
            b = t % 2
            X, xk = X32[t % 3], f"X32_{t % 3}"
            b3 = t % 3
            SM = SMF[b3]
            smk = lambda n: f"SMF{b3}_{n}"
            act(XN16[:], X[:], AF.Square, [xk], ["XN16", smk("ssx")], accum=SM[:, 0:1])
            rsqrt_small(SM[:, 1:2], SM[:, 0:1], [smk("ssx")], [smk("rx")], scale_in=1.0 / D)
            ts("dve", XN16[:], X[:], SM[:, 1:2], None, ALU.mult, None, [xk, smk("rx")], ["XN16"])
            P, pk = pb()
            for kc in range(8):
                tr(P[:, kc * 128:(kc + 1) * 128], XN16[:, kc * 128:(kc + 1) * 128], ["XN16"], [pk])
            cp("act", XT16[:].rearrange("p k t -> p (k t)"), P[:, :], [pk], ["XT16"])
            S.cut()
            R, rk_ = RAW16[b], f"RAW16_{b}"
            Rp, rpk = RAW16[1 - b], f"RAW16_{1 - b}"
            cp("pool", R[:, :, 0:3], Rp[:, :, 128:131], [rpk + "_0", rpk + "_1", rpk + "_2"], [rk_ + "_h"])
            for grp in range(3):
                P, pk = pf()
                for ci in range(4):
                    c0 = C_Q + (grp * 4 + ci) * 128
                    for kc in range(8):
                        mm(P[:, ci * 128:(ci + 1) * 128], WIN16[:, kc, c0:c0 + 128], XT16[:, kc, :], ["WIN16", "XT16"], [pk],
                           start=(kc == 0), stop=(kc == 7))
                    if ci % 2 == 1:
                        S.cut()
                cp("dve" if grp == 0 else "act", R[:, grp * 4:(grp + 1) * 4, 3:131], v4(P[:, :]), [pk], [rk_ + f"_{grp}"])
            P, pk = pf()
            for ci in range(4):
                c0 = C_PZ + ci * 128
                for kc in range(8):
                    mm(P[:, ci * 128:(ci + 1) * 128], WIN16[:, kc, c0:c0 + 128], XT16[:, kc, :], ["WIN16", "XT16"], [pk],
                       start=(kc == 0), stop=(kc == 7))
                if ci % 2 == 1:
                    S.cut()
            act(SZ16[:], P[:, :], AF.Silu, [pk], ["SZ16"])
            P, pk = pf()
            for kc in range(8):
                mm(P[:, 0:512], XT16[:, kc, :], WIN16[:, kc, C_PU:C_PU + 512], ["WIN16", "XT16"], [pk],
                   start=(kc == 0), stop=(kc == 7))
            S.cut()
            Uc, uk = U16[b], f"U16_{b}"
            Up, upk = U16[1 - b], f"U16_{1 - b}"
            cp("dve", Uc[:], P[:, :], [pk], [uk])
            P, pk = pf()
            for kc in range(8):
                mm(P[:, 0:512], XT16[:, kc, :], WIN16[:, kc, C_DZ:C_DZ + 512], ["WIN16", "XT16"], [pk],
                   start=(kc == 0), stop=(kc == 7))
            S.cut()
            act(GATE[b3][:], P[:, :], AF.Silu, [pk], [f"GATE_{b3}"])
            P, pk = pf()
            for kc in range(8):
                mm(P[:, 0:8], XT16[:, kc, :], WIN16[:, kc, C_BA:C_BA + 8], ["WIN16", "XT16"], [pk],
                   start=(kc == 0), stop=(kc == 7))
            cp("dve", SM[:, 8:16], P[:, 0:8], [pk], [smk("ba")])
            S.cut()
            for grp in range(3):
                P, pk = pf()
                for ci in range(4):
                    ch = grp * 4 + ci
                    for j in range(4):
                        mm(P[:, ci * 128:(ci + 1) * 128], DIAG16[:, ch * 4 + j, :], R[:, ch, j:j + 128], [f"DIAG16_{ch * 4 + j}", rk_ + "_h", rk_ + f"_{grp}"], [pk],
                           start=(j == 0), stop=(j == 3))
                act(QKV16[:, grp * 4:(grp + 1) * 4, :].rearrange("p c t -> p (c t)"), P[:, :], AF.Silu, [pk], [f"QKV16_{grp}"])
                S.cut()
            Pm, pkm = pf()
            for g in range(4):
                gs = slice(g * 128, (g + 1) * 128)
                if t == 0:
                    mm(Pm[:, gs], Uc[:, gs], WCT16[:, gs], [uk, "WCT16"], [pkm], start=True, stop=False)
                    mm(Pm[:, gs], Uc[:, gs], WC0HI16[:, gs], [uk, "WC0HI16"], [pkm], start=False, stop=False)
                    mm(Pm[:, gs], Uc[:, gs], WC0LO16[:, gs], [uk, "WC0LO16"], [pkm], start=False, stop=True)
                else:
                    mm(Pm[:, gs], Up[:, gs], WPT16[:, gs], [upk, "WPT16"], [pkm], start=True, stop=False)
                    mm(Pm[:, gs], Uc[:, gs], WCT16[:, gs], [uk, "WCT16"], [pkm], start=False, stop=True)
            cp("act", MIX16[:], Pm[:, :], [pkm], ["MIX16"])
            S.cut()
            Pp, pkp = pf()
            for g in range(4):
                gs = slice(g * 128, (g + 1) * 128)
                mm(Pp[:, gs], POOLW16[:, g, :], MIX16[:, gs], ["POOLW16", "MIX16"], [pkp])
            tt("dve", YTP16[b3][:], Pp[:, :], SZ16[:], ALU.mult, [pkp, "SZ16"], [f"YTP16_{b3}"])
            S.cut()
            tt("dve", SQ16[:], QKV16[:, 0:8, :], QKV16[:, 0:8, :], ALU.mult, ["QKV16_0", "QKV16_1"], ["SQ16"])
            P, pk = pf()
            for c in range(8):
                mm(P[:, c:c + 1], SQ16[:, c, :], ONES16[:, 0:1], ["SQ16", "ONES16"], [pk])
            cp("dve", SM[:, 16:24], P[:, 0:8], [pk], [smk("ssqk")])
            S.cut()
            rsqrt_small(SM[:, 24:32], SM[:, 16:24], [smk("ssqk")], [smk("rqk")])
            ts("dve", SM[:, 24:28], SM[:, 24:28], float(128 ** -0.5), None, ALU.mult, None, [smk("rqk")], [smk("rqk")])
            act(SM[:, 32:36], SM[:, 8:12], AF.Exp, [smk("ba")], [smk("beta")], scale=-1.0)
            ts("dve", SM[:, 32:36], SM[:, 32:36], 1.0, None, ALU.add, None, [smk("beta")], [smk("beta")])
            S.op("dve", lambda e: e.reciprocal(out=SM[:, 32:36], in_=SM[:, 32:36]), [smk("beta")], [smk("beta")])
            tt("dve", SM[:, 36:40], SM[:, 12:16], DTB[:, 0:4], ALU.add, [smk("ba"), "DTB"], [smk("g")])
            act(SM[:, 36:40], SM[:, 36:40], AF.Exp, [smk("g")], [smk("g")])
            act(SM[:, 36:40], SM[:, 36:40], AF.Ln, [smk("g"), "ONEC"], [smk("g")], bias=ONEC[:, 0:1])
            tt("dve", SM[:, 36:40], SM[:, 36:40], NEGA[:, 0:4], ALU.mult, [smk("g"), "NEGA"], [smk("g")])
            P, pk = pf()
            mm(P[:, 0:4], C32["u2"][:], SM[:, 36:40], ["C_u2", smk("g")], [pk])
            mm(P[:, 4:8], C32["cisame"][:], SM[:, 36:40], ["C_cisame", smk("g")], [pk])
            mm(P[:, 8:12], C32["ci0"][:], SM[:, 36:40], ["C_ci0", smk("g")], [pk])
            mm(P[:, 12:16], C32["ci1"][:], SM[:, 36:40], ["C_ci1", smk("g")], [pk])
            cp("dve", SM[:, 40:56], P[:, 0:16], [pk], [smk("gc"), smk("gc2")])
            tt("dve", SM[:, 44:48], SM[:, 44:48], SM[:, 40:44], ALU.subtract, [smk("gc")], [smk("gc2")])
            act(SM[:, 56:72], SM[:, 40:56], AF.Exp, [smk("gc"), smk("gc2")], [smk("ex")])
            RK = SM[:, 28:32]
            BETA = SM[:, 32:36]
            tt("dve", SM[:, 84:88], RK, BETA, ALU.mult, [smk("rqk"), smk("beta")], [smk("c0")])
            stt("dve", SM[:, 72:76], SM[:, 84:88], -1.0, RK, ALU.mult, ALU.mult, [smk("c0"), smk("rqk")], [smk("c1")])
            stt("dve", SM[:, 76:80], SM[:, 72:76], -1.0, SM[:, 56:60], ALU.mult, ALU.mult, [smk("c1"), smk("ex")], [smk("c2")])
            S.cut()
            Pk, pkk = pb()
            for h in range(4):
                tr(Pk[:, h * 128:(h + 1) * 128], QKV16[:, 4 + h, :], ["QKV16_1"], [pkk])
                tr(Pk[:, 512 + h * 128:512 + (h + 1) * 128], QKV16[:, 8 + h, :], ["QKV16_2"], [pkk])
            tt("dve", v4(KDEC16[b3][:]), v4(Pk[:, 0:512]), bc4(SM[:, 60:64]), ALU.mult, [pkk, smk("ex")], [f"KDEC16_{b3}"])
            tt("dve", v4(KBA16[b][:]), v4(Pk[:, 0:512]), bc4(SM[:, 76:80]), ALU.mult, [pkk, smk("c2")], [f"KBA16_{b}"])
            tt("dve", v4(VB16[b][:]), v4(Pk[:, 512:1024]), bc4(SM[:, 84:88]), ALU.mult, [pkk, smk("c0")], [f"VB16_{b}"])
            S.cut()
            tt("dve", v4(GU[:]), v4(C32["u2rep"][:]), bc4(SM[:, 36:40]), ALU.mult, ["C_u2rep", smk("g")], ["GU"])
            Pg, pkg = pf()
            mm(Pg[:, :], ONES128[:], GU[:], ["ONES128", "GU"], [pkg])
            ts("dve", SM[:, 88:92], SM[:, 40:44], -1.0, None, ALU.mult, None, [smk("gc")], [smk("ngc")])
            for h in range(4):
                act(DA[:, h * 128:(h + 1) * 128], Pg[:, h * 128:(h + 1) * 128], AF.Abs, [pkg, smk("ngc")], ["DA"],
                    bias=SM[:, 88 + h:89 + h])
            act(AROW[:], Pg[:, :], AF.Exp, [pkg], ["AROW"])
            act(EE[:], DA[:], AF.Exp, ["DA"], ["EE16"], scale=-1.0)
            S.cut()
            tt("dve", GT[:], EE[:], MU16[:], ALU.mult, ["EE16", "MU16"], ["GT"])
            tt("dve", GSB[:], EE[:], MS16[:], ALU.mult, ["EE16", "MS16"], ["GSB"])
            tt("dve", QDT16[b3][:], QKV16[:, 0:4, :].rearrange("p c t -> p (c t)"), AROW[:], ALU.mult, ["QKV16_0", "AROW"], [f"QDT16_{b3}"])
            Pkk, pkkk = pf()
            for h in range(4):
                mm(Pkk[:, h * 128:(h + 1) * 128], QKV16[:, 4 + h, :], QKV16[:, 4 + h, :], ["QKV16_1"], [pkkk])
            Pqk, pkqk = pf()
            for h in range(4):
                mm(Pqk[:, h * 128:(h + 1) * 128], QKV16[:, 4 + h, :], QKV16[:, h, :], ["QKV16_1", "QKV16_0"], [pkqk])
            S.cut()
            for h in range(4):
                hs = slice(h * 128, (h + 1) * 128)
                stt("dve", PB0[b][:, hs], Pkk[:, hs], SM[:, 72 + h:73 + h], GSB[:, hs], ALU.mult, ALU.mult,
                    [pkkk, "GSB", smk("c1")], [f"PB0_{b}"])
            tt("dve", QKT16[b3][:], Pqk[:, :], GT[:], ALU.mult, [pkqk, "GT"], [f"QKT16_{b3}"])
            P3, pk3 = pb()
            for h in range(4):
                tr(P3[:, h * 128:(h + 1) * 128], PB0[b][:, h * 128:(h + 1) * 128], [f"PB0_{b}"], [pk3])
            cp("act", PH0[b][:, :, :], v4(P3[:, 0:512]), [pk3], [f"PH0_{b}"])
            tt("dve", PH1[b][:, :, 128:256], v4(P3[:, 0:512]), ID16[:].unsqueeze(1).to_broadcast([128, 4, 128]), ALU.add,
               [pk3, "ID16"], [f"PH1_{b}_H"])
            S.cut()

        def stageB(t):
            stage_[0] = "B"
            b = t % 2
            X, xk = X32[t % 3], f"X32_{t % 3}"
            SM = SMF[b]
            smk = lambda n: f"SMF{b}_{n}"
            seq = [(PB0[b], f"PB0_{b}", PH0[b], f"PH0_{b}", None), (PBX[0], "PBX_0", PH1[b], f"PH1_{b}_P", f"PH1_{b}_H"),
                   (PBX[1], "PBX_1", PHX[1], "PHX_1_P", "PHX_1_H"), (PBX[0], "PBX_0", PHX[0], "PHX_0_P", "PHX_0_H"),
                   (PBX[1], "PBX_1", PHX[1], "PHX_1_P", "PHX_1_H"), (PBX[0], "PBX_0", PHX[0], "PHX_0_P", "PHX_0_H")]
            for k in range(5):
                PBc, kb, PHc, khP, khH = seq[k]
                PBn, kbn, PHn, khnP, khnH = seq[k + 1]
                Pa, pka = pf()
                for h in range(4):
                    mm(Pa[:, h * 128:(h + 1) * 128], PHc[:, h, 0:128], PBc[:, h * 128:(h + 1) * 128], [khP, kb], [pka])
                cp("act", PBn[:], Pa[:, :], [pka], [kbn])
                if k == 0:
                    Pb_, pkb = pf()
                    for h in range(4):
                        mm(Pb_[:, h * 128:(h + 1) * 128], PBc[:, h * 128:(h + 1) * 128], PHc[:, h, 0:128], [kb, khP], [pkb])
                    cp("dve", PHn[:, :, 0:128], v4(Pb_[:, :]), [pkb], [khnP])
                elif k < 4:
                    Pb0, pkb0 = pf()
                    Pb1, pkb1 = pf()
                    for h in range(4):
                        Pd = Pb0 if h < 2 else Pb1
                        pkd = pkb0 if h < 2 else pkb1
                        mm(Pd[:, (h % 2) * 256:(h % 2) * 256 + 256], PBc[:, h * 128:(h + 1) * 128], PHc[:, h, :], [kb, khP, khH], [pkd])
                    for half, (Pd, pkd) in enumerate(((Pb0, pkb0), (Pb1, pkb1))):
                        pv = Pd[:, :].rearrange("p (h c) -> p h c", h=2)
                        hs = slice(half * 2, half * 2 + 2)
                        cp("act", PHn[:, hs, 0:128], pv[:, :, 0:128], [pkd], [khnP])
                        tt("dve", PHn[:, hs, 128:256], pv[:, :, 128:256], PHc[:, hs, 128:256], ALU.add, [pkd, khH], [khnH])
                else:
                    Pb_, pkb = pf()
                    for h in range(4):
                        mm(Pb_[:, h * 128:(h + 1) * 128], PBc[:, h * 128:(h + 1) * 128], PHc[:, h, 128:256], [kb, khH], [pkb])
                    tt("dve", PHn[:, :, 128:256], v4(Pb_[:, :]), PHc[:, :, 128:256], ALU.add, [pkb, khH], [khnH])
                S.cut()
            PBc, kb, PHc, khP, khH = seq[5]
            Pb_, pkb = pf()
            for h in range(4):
                mm(Pb_[:, h * 128:(h + 1) * 128], PBc[:, h * 128:(h + 1) * 128], PHc[:, h, 128:256], [kb, khH], [pkb])
            tt("dve", v4(HF16[:]), v4(Pb_[:, :]), PHc[:, :, 128:256], ALU.add, [pkb, khH], ["HF16"])
            S.cut()
            Pu, pku = pf()
            for h in range(4):
                mm(Pu[:, h * 128:(h + 1) * 128], HF16[:, h * 128:(h + 1) * 128], VB16[b][:, h * 128:(h + 1) * 128], ["HF16", f"VB16_{b}"], [pku])
            cp("act", U32[b][:], Pu[:, :], [pku], [f"U32_{b}"])
            Pw, pkw = pf()
            for h in range(4):
                mm(Pw[:, h * 128:(h + 1) * 128], KBA16[b][:, h * 128:(h + 1) * 128], HF16[:, h * 128:(h + 1) * 128], [f"KBA16_{b}", "HF16"], [pkw])
            cp("dve", WT16[b][:], Pw[:, :], [pkw], [f"WT16_{b}"])
            S.cut()
        def stageC(t):
            stage_[0] = "C"
            b = t % 2
            b3 = t % 3
            X, xk = X32[t % 3], f"X32_{t % 3}"
            SM = SMF[b3]
            smk = lambda n: f"SMF{b3}_{n}"
            for c in range(2):
                rs = slice(64 * c, 64 * c + 64)
                VN, vnk = VNZ[c], f"VNZ_{c}"
                Pr, pkr = pf()
                for h in range(4):
                    hs = slice(h * 128, (h + 1) * 128)
                    mm(Pr[:, hs], WT16[b][:, hs], S16[:, hs], [f"WT16_{b}", "S16"], [pkr])
                tt("dve", VN[rs, :], U32[b][rs, :], Pr[rs, :], ALU.subtract, [f"U32_{b}", pkr], [vnk])
                S.cut()
                Po, pko = pf()
                for h in range(4):
                    hs = slice(h * 128, (h + 1) * 128)
                    mm(Po[:, hs], QDT16[b3][:, hs], S16[:, hs], [f"QDT16_{b3}", "S16"], [pko], start=True, stop=False)
                    mm(Po[:, hs], QKT16[b3][:, hs], VN[:, hs], [f"QKT16_{b3}", vnk], [pko], start=False, stop=True)
                cp("act", ORAW[rs, :], Po[rs, :], [pko], ["ORAW"])
                Ps, pks = pf()
                for h in range(4):
                    hs = slice(h * 128, (h + 1) * 128)
                    mm(Ps[:, hs], KDEC16[b3][:, hs], VN[:, hs], [f"KDEC16_{b3}", vnk], [pks])
                for h in range(4):
                    hs = slice(h * 128, (h + 1) * 128)
                    stt("dve", S32[:, hs], S32[:, hs], SM[:, 64 + 4 * c + h:65 + 4 * c + h], Ps[:, hs], ALU.mult, ALU.add,
                        ["S32", smk("ex"), pks], ["S32"])
                cp("act", S16[:], S32[:], ["S32"], ["S16"])
                S.cut()
            for h in range(4):
                act(SQO[:, h * 128:(h + 1) * 128], ORAW[:, h * 128:(h + 1) * 128], AF.Square, ["ORAW"], ["SQO", "SMC_sso"],
                    accum=SMC[:, h:h + 1])
            RQ = SM[:, 24:28]
            tt("dve", SMC[:, 4:8], RQ, RQ, ALU.mult, [smk("rqk")], ["SMC_o1"])
            stt("dve", SMC[:, 4:8], SMC[:, 4:8], 1.0 / 128.0, SMC[:, 0:4], ALU.mult, ALU.mult, ["SMC_o1", "SMC_sso"], ["SMC_o1"])
            rsqrt_small(SMC[:, 8:12], SMC[:, 4:8], ["SMC_o1"], ["SMC_o2"])
            tt("dve", SMC[:, 8:12], SMC[:, 8:12], RQ, ALU.mult, ["SMC_o2", smk("rqk")], ["SMC_o2"])
            for h in range(4):
                hs = slice(h * 128, (h + 1) * 128)
                stt("dve", YDN16[:, hs], ORAW[:, hs], SMC[:, 8 + h:9 + h], GATE[b3][:, hs], ALU.mult, ALU.mult,
                    ["ORAW", "SMC_o2", f"GATE_{b3}"], ["YDN16"])
            S.cut()
            P4, pk4 = pb()
            for h in range(4):
                tr(P4[:, h * 128:(h + 1) * 128], YDN16[:, h * 128:(h + 1) * 128], ["YDN16"], [pk4])
            cp("act", YTD16[:], P4[:, 0:512], [pk4], ["YTD16"])
            S.cut()
            for nh in range(2):
                Pq, pkq = pf()
                for e8 in range(8):
                    lhs = YTP16[b3][:, e8 * 128:(e8 + 1) * 128] if e8 < 4 else YTD16[:, (e8 - 4) * 128:(e8 - 3) * 128]
                    mm(Pq[:, 0:512], lhs, WOUT16[:, e8, nh * 512:(nh + 1) * 512],
                       [f"YTP16_{b3}" if e8 < 4 else "YTD16", "WOUT16"], [pkq], start=(e8 == 0), stop=(e8 == 7))
                tt("dve", H32[:, nh * 512:(nh + 1) * 512], Pq[:, :], X[:, nh * 512:(nh + 1) * 512], ALU.add, [pkq, xk], ["H32"])
            if t + 3 < n_tiles:
                load_x(t + 3)
            act(SQO[:].bitcast(BF16), H32[:], AF.Square, ["H32"], ["SQO", "SMC_ssh"], accum=SMC[:, 12:13])
            rsqrt_small(SMC[:, 13:14], SMC[:, 12:13], ["SMC_ssh"], ["SMC_rh"], scale_in=1.0 / D)
            O, ok = OUT32[b], f"OUT32_{b}"
            stt(cfg.get("out_eng", "dve"), O[:], H32[:], SMC[:, 13:14], FNW[:], ALU.mult, ALU.mult, ["H32", "SMC_rh", "FNW"], [ok])
            S.dma("sp", out_v[t], O[:], [ok], [], f"st_OUT32_{b}")
            S.cut()

        for t0 in range(min(3, n_tiles)):
            load_x(t0)
        for t in range(n_tiles):
            front(t)
            stageB(t)
            stageC(t)
        if cfg.get("junk"):
            g_, f_, m_ = cfg["junk"]

            jb_ = cfg.get("junk_bank", 7)

            def _mk_junk():
                o_ = _Op("pe", lambda e: e.matmul(PSF[jb_][:, 0:128], lhsT=ID16[:], rhs=ID16[:], start=True, stop=True,
                                                  skip_group_check=True), ["ID16"], [])
                return o_
            Sched.JUNK = (g_, f_, m_, _mk_junk)
        else:
            Sched.JUNK = None
        Sched.ACT_PEN = float(cfg.get("act_pen", 1400.0))
        Sched.JITTER = tuple(cfg["jitter"]) if cfg.get("jitter") else None
        Sched.ATTACH_WAIT = bool(cfg.get("attach_wait", True))
        Sched.XLAT = float(cfg.get("xlat", 710.0))
        S.schedule()
        print("[sched] est_total_us=%.1f n_ops=%d" % (S.est_total / 1e3, len(S.ops)))
        S.emit(sems)
    return nc, list(dbg_out.keys())


def _core_inputs(inputs, b, consts):
    f = lambda a: np.ascontiguousarray(np.asarray(a, dtype=np.float32))
    conv_w = np.asarray(inputs["conv_w"][0], np.float32)
    m = {
        "x": f(inputs["x"][b]),
        "w_in": f(inputs["w_in"][0]),
        "w_out": f(inputs["w_out"][0]),
        "pool_w": f(np.transpose(np.asarray(inputs["pool_w"][0]), (1, 0, 2))),
        "pool_scale": f(np.asarray(inputs["pool_scale"][0]).reshape(512)),
        "conv_w": f(np.transpose(conv_w.reshape(4, 12, 128), (2, 1, 0)).reshape(128, 48)),
        "a_log": f(inputs["a_log"][0]),
        "dt_bias": f(inputs["dt_bias"][0]),
        "dn_norm_w": f(np.asarray(inputs["dn_norm_w"][0]).reshape(128, 1)),
        "norm_w": f(np.asarray(inputs["norm_w"][0]).reshape(8, 128).T),
        "final_norm_w": f(inputs["final_norm_w"]),
    }
    for k, v in consts.items():
        m["c_" + k] = f(v)
    return m


def kernel(**inputs):
    consts = _consts()
    nc, _ = build_nc()
    in_maps = [_core_inputs(inputs, b, consts) for b in range(8)]
    res = run_bass_kernel_spmd(nc, in_maps, core_ids=list(range(8)))
    out = np.stack([np.asarray(r["out"], dtype=np.float32) for r in res.results], axis=0)
    return out
```

```python
import contextlib
import numpy as np
import ml_dtypes
import concourse.bass as bass
import concourse.mybir as mybir
from concourse.bass_utils import run_bass_kernel_spmd

F32 = mybir.dt.float32
BF16 = mybir.dt.bfloat16
AF = mybir.ActivationFunctionType
ALU = mybir.AluOpType
AX = mybir.AxisListType

SEQ = 4096
D = 1024
DIN = 3080
NT = SEQ // 128
EPS = 1e-6
POOL_WINDOWS = (2, 4, 8, 16)
C_PU, C_PZ, C_Q, C_K, C_V, C_DZ, C_BA = 0, 512, 1024, 1536, 2048, 2560, 3072


class _Op:
    __slots__ = ("eng", "fn", "reads", "writes", "dma_key", "idx", "deps", "sig", "dma_val", "waits", "cost", "tset")

    def __init__(self, eng, fn, reads, writes, dma_key=None):
        self.eng = eng
        self.fn = fn
        self.reads = tuple(reads)
        self.writes = tuple(writes)
        self.dma_key = dma_key
        self.sig = None
        self.dma_val = None
        self.cost = 100.0
        self.tset = None


class Sched:
    COMPUTE = ("pe", "act", "dve", "pool")
    XLAT = 250.0
    NO_WAR = 0
    ACT_PEN = 1400.0
    JITTER = None
    ATTACH_WAIT = False
    JUNK = None
    PS_BIAS = 0.0

    def __init__(self, nc):
        self.nc = nc
        self.ops = []
        self.cuts = []

    def cut(self):
        self.cuts.append(len(self.ops))

    def op(self, eng, fn, reads=(), writes=()):
        extra = [r for r in reads if r.startswith("ps") and r not in writes]
        if extra:
            writes = list(writes) + extra
        o = _Op(eng, fn, reads, writes)
        self.ops.append(o)
        return o

    def dma(self, queue, out, in_, reads, writes, key):
        o = _Op(queue, lambda e: e.dma_start(out=out, in_=in_), reads, writes, dma_key=key)
        nb = 4
        for d_ in out.shape:
            nb *= int(d_)
        o.cost = 2200.0 + nb / 200.0
        self.ops.append(o)
        return o

    def schedule(self):
        ops = self.ops
        n = len(ops)
        last_w, readers = {}, {}
        preds = [set() for _ in range(n)]
        for i, o in enumerate(ops):
            for r in o.reads:
                w = last_w.get(r)
                if w is not None:
                    preds[i].add(w)
            for wk in o.writes:
                w = last_w.get(wk)
                if w is not None:
                    preds[i].add(w)
                if not (Sched.NO_WAR and (Sched.NO_WAR == 2 or not wk.startswith("ps"))):
                    for rd in readers.get(wk, ()):
                        preds[i].add(rd)
            preds[i].discard(i)
            for r in o.reads:
                readers.setdefault(r, set()).add(i)
            for wk in o.writes:
                last_w[wk] = i
                readers[wk] = set()
        succs = [[] for _ in range(n)]
        npred = [len(p) for p in preds]
        for i, p in enumerate(preds):
            for j in p:
                succs[j].append(i)
        free = {}
        rtime = [0.0] * n
        ready = set(i for i in range(n) if npred[i] == 0)
        order = []
        self.trace = []
        jit = None
        if Sched.JITTER is not None:
            import random as _random
            rng_ = _random.Random(Sched.JITTER[0])
            jit = [rng_.uniform(0.0, Sched.JITTER[1]) for _ in range(n)]
        open_ps = {}
        self.bind = {}
        self.seq_ops = list(ops)
        act_set = None
        XLAT = Sched.XLAT
        while ready:
            best, bk = None, None
            for i in ready:
                o = ops[i]
                st = max(free.get(o.eng, 0.0), rtime[i])
                if o.eng == "act" and o.tset is not None and o.tset != act_set:
                    st += Sched.ACT_PEN
                pr = st + (jit[i] if jit is not None else 0.0)
                if Sched.PS_BIAS:
                    for kk_ in o.writes:
                        if kk_.startswith("ps") and open_ps.get(kk_, 0) > 0:
                            pr = st - Sched.PS_BIAS
                            break
                k = (pr, i, st)
                if bk is None or k < bk:
                    best, bk = i, k
            i = best
            o = ops[i]
            ready.discard(i)
            st = bk[2]
            if o.eng == "pe" and Sched.JUNK is not None:
                gap_ = st - free.get("pe", 0.0)
                if gap_ > Sched.JUNK[0] and free.get("pe", 0.0) > 0.0:
                    for _ in range(min(int(gap_ * Sched.JUNK[1] / 60.0), Sched.JUNK[2])):
                        order.append(Sched.JUNK[3]())
            for kk_ in o.writes:
                if kk_.startswith("ps"):
                    if o.eng == "pe":
                        open_ps[kk_] = open_ps.get(kk_, 0) + 1
                    else:
                        open_ps[kk_] = 0
            if o.eng == "act" and o.tset is not None:
                act_set = o.tset
            if o.dma_key is not None:
                free[o.eng] = st + 60.0
                fin = st + o.cost
            else:
                fin = st + o.cost
                free[o.eng] = fin
            order.append(o)
            self.trace.append((i, o.eng, st, fin, rtime[i], o.cost))
            for j in succs[i]:
                lat = XLAT if (ops[j].eng != o.eng or o.dma_key is not None) else 60.0
                if fin + lat > rtime[j]:
                    rtime[j] = fin + lat
                    self.bind[j] = i
                npred[j] -= 1
                if npred[j] == 0:
                    ready.add(j)
        self.ops = order
        self.est_total = max(free.values()) if free else 0.0

    def emit(self, sems):
        nc = self.nc
        last_w = {}
        readers = {}
        dma_cnt = {}
        for i, o in enumerate(self.ops):
            o.idx = i
            deps = set()
            for r in o.reads:
                w = last_w.get(r)
                if w is not None:
                    deps.add(w)
            for wk in o.writes:
                w = last_w.get(wk)
                if w is not None:
                    deps.add(w)
                for rd in readers.get(wk, ()):
                    deps.add(rd)
            deps.discard(o)
            o.deps = deps
            for r in o.reads:
                readers.setdefault(r, set()).add(o)
            for wk in o.writes:
                last_w[wk] = o
                readers[wk] = set()
            if o.dma_key is not None:
                dma_cnt[o.dma_key] = dma_cnt.get(o.dma_key, 0) + 1
                o.dma_val = 16 * dma_cnt[o.dma_key]
        needed = set()
        for o in self.ops:
            best = {}
            dma_deps = []
            for d in o.deps:
                if d.dma_key is not None:
                    dma_deps.append(d)
                    continue
                if d.eng == "pe" and o.eng == "pe":
                    continue
                b = best.get(d.eng)
                if b is None or d.idx > b.idx:
                    best[d.eng] = d
            o.deps = list(best.values()) + dma_deps
            for d in best.values():
                needed.add(d)
        cnt = {e: 0 for e in self.COMPUTE}
        for o in self.ops:
            if o in needed:
                cnt[o.eng] += 1
                o.sig = cnt[o.eng]
        waited = {}
        for o in self.ops:
            w = []
            for d in o.deps:
                if d.dma_key is not None:
                    k = ("dma", d.dma_key)
                    v = d.dma_val
                    s = sems[d.dma_key]
                else:
                    k = d.eng
                    v = d.sig
                    s = sems[d.eng]
                cur = waited.setdefault(o.eng, {}).get(k, 0)
                if cur >= v:
                    continue
                waited[o.eng][k] = v
                w.append((s, v))
            o.waits = w
        final_dma = {k: 16 * c for k, c in dma_cnt.items()}
        self.sig_counts = cnt
        by_eng = {}
        for o in self.ops:
            by_eng.setdefault(o.eng, []).append(o)
        engmap = {"pe": "tensor", "act": "scalar", "dve": "vector", "pool": "gpsimd", "sp": "sync"}

        def make(engname, ops):
            def body(eng):
                for o in ops:
                    wl = o.waits
                    attach = None
                    if Sched.ATTACH_WAIT and wl and o.dma_key is None:
                        attach = wl[-1]
                        wl = wl[:-1]
                    for (s, v) in wl:
                        eng.wait_ge(s, v)
                    ins = o.fn(eng)
                    if attach is not None:
                        ins._wait_ge(attach[0], attach[1])
                    if o.dma_key is not None:
                        ins.then_inc(sems[o.dma_key], 16)
                    elif o.sig is not None:
                        ins.then_inc(sems[o.eng], 1)
                if engname == "sp":
                    for k, v in final_dma.items():
                        eng.wait_ge(sems[k], v)
            return body

        with nc.Block() as block:
            for engname in ("sp", "pe", "act", "dve", "pool"):
                ops = by_eng.get(engname, [])
                if not ops and engname != "sp":
                    continue
                getattr(block, engmap[engname])(make(engname, ops))


def _consts():
    c = {}
    idx = np.arange(128)
    same = (idx[:, None] // 64) == (idx[None, :] // 64)
    c["ident32"] = np.eye(128, dtype=np.float32)
    c["u2"] = (same & (idx[:, None] <= idx[None, :])).astype(np.float32)
    c["u2rep"] = np.tile(c["u2"], (1, 4))
    c["cisame"] = same.astype(np.float32)
    c["ci0"] = np.repeat((idx < 64).astype(np.float32)[:, None], 128, axis=1)
    c["ci1"] = np.repeat((idx >= 64).astype(np.float32)[:, None], 128, axis=1)
    ms = (same & (idx[None, :] < idx[:, None])).astype(np.float32)
    mu = (same & (idx[None, :] >= idx[:, None])).astype(np.float32)
    c["ms"] = np.tile(ms, (1, 4))
    c["mu"] = np.tile(mu, (1, 4))
    c["ones32"] = np.ones((128, 1), np.float32)
    wct = np.zeros((128, 4, 128), np.float32)
    wpt = np.zeros((128, 4, 128), np.float32)
    wc0 = np.zeros((128, 4, 128), np.float32)
    for g, w in enumerate(POOL_WINDOWS):
        for t in range(128):
            for j in range(t - w + 1, t + 1):
                if j >= 0:
                    wct[j, g, t] += 1.0 / w
                else:
                    wpt[128 + j, g, t] += 1.0 / w
            wct[t, g, t] -= 1.0
            cnt = min(t + 1, w)
            for j in range(max(0, t - w + 1), t + 1):
                wc0[j, g, t] += 1.0 / cnt
            wc0[t, g, t] -= 1.0
    corr = wc0 - wct
    hi = corr.astype(ml_dtypes.bfloat16).astype(np.float32)
    lo = (corr - hi).astype(ml_dtypes.bfloat16).astype(np.float32)
    c["wct"] = wct.reshape(128, 512)
    c["wpt"] = wpt.reshape(128, 512)
    c["wc0hi"] = hi.reshape(128, 512)
    c["wc0lo"] = lo.reshape(128, 512)
    return c


CONST_SHAPES = {"ident32": [128, 128], "u2": [128, 128], "u2rep": [128, 512], "cisame": [128, 128], "ci0": [128, 128],
                "ci1": [128, 128], "ms": [128, 512], "mu": [128, 512], "ones32": [128, 1],
                "wct": [128, 512], "wpt": [128, 512], "wc0hi": [128, 512], "wc0lo": [128, 512]}


def build_nc(n_tiles=NT, dbg_tile=None, cfg=None):
    cfg = cfg or {}
    nc = bass.Bass("TRN2", target_bir_lowering=False)
    dram = {}

    def din(name, shape):
        dram[name] = nc.dram_tensor(name, list(shape), F32, kind="ExternalInput").ap()
        return dram[name]

    x_d = din("x", [SEQ, D])
    win_d = din("w_in", [D, DIN])
    wout_d = din("w_out", [D, D])
    poolw_d = din("pool_w", [128, 4, 128])
    psrow_d = din("pool_scale", [512])
    convw_d = din("conv_w", [128, 48])
    alog_d = din("a_log", [4])
    dtb_d = din("dt_bias", [4])
    dnwc_d = din("dn_norm_w", [128, 1])
    nw_d = din("norm_w", [128, 8])
    fnw_d = din("final_norm_w", [D])
    cd = {k: din("c_" + k, s) for k, s in CONST_SHAPES.items()}
    out_d = nc.dram_tensor("out", [SEQ, D], F32, kind="ExternalOutput").ap()
    dbg_out = {}

    S = Sched(nc)
    with contextlib.ExitStack() as es:
        def sb(name, shape, dt=F32):
            return es.enter_context(nc.sbuf_tensor(name, list(shape), dt))

        def psum(name, shape, dt):
            return es.enter_context(nc.psum_tensor(name, list(shape), dt))

        sems = {}

        def sem(k):
            if k not in sems:
                sems[k] = es.enter_context(nc.semaphore("s_" + k))
            return k

        for e in ("pe", "act", "dve", "pool"):
            sem(e)

        def act(out, in_, func, reads, writes, bias=None, scale=None, accum=None):
            kw = {}
            if bias is not None:
                kw["bias"] = bias
            if scale is not None:
                kw["scale"] = scale
            if accum is not None:
                kw["accum_out"] = accum
            o_ = S.op("act", lambda e: e.activation(out=out, in_=in_, func=func, **kw), reads, writes)
            o_.cost = 190.0 + 0.83 * fsz(out)
            o_.tset = {AF.Silu: "silu", AF.Exp: "lnexp", AF.Ln: "lnexp"}.get(func)
            return o_

        def fsz(ap):
            n_ = 1
            for d_ in ap.shape[1:]:
                n_ *= int(d_)
            return n_

        def ecost(eng, out):
            n_ = fsz(out)
            if eng == "act":
                return 190.0 + 0.83 * n_
            if eng == "pool":
                return 120.0 + 2.2 * n_
            return 110.0 + 0.8 * n_

        def cp(eng, out, in_, reads, writes):
            if eng == "act":
                o_ = S.op("act", lambda e: e.copy(out=out, in_=in_), reads, writes)
            else:
                o_ = S.op(eng, lambda e: e.tensor_copy(out=out, in_=in_), reads, writes)
            o_.cost = ecost(eng, out)
            return o_

        def tt(eng, out, in0, in1, op, reads, writes):
            o_ = S.op(eng, lambda e: e.tensor_tensor(out=out, in0=in0, in1=in1, op=op), reads, writes)
            o_.cost = ecost(eng, out) + (800.0 if op == ALU.pow else 0.0)
            return o_

        def stt(eng, out, in0, scalar, in1, op0, op1, reads, writes):
            o_ = S.op(eng, lambda e: e.scalar_tensor_tensor(out=out, in0=in0, scalar=scalar, in1=in1,
                                                            op0=op0, op1=op1), reads, writes)
            o_.cost = ecost(eng, out) * 1.1
            return o_

        def ts(eng, out, in0, s1, s2, op0, op1, reads, writes):
            if s2 is None:
                o_ = S.op(eng, lambda e: e.tensor_scalar(out=out, in0=in0, scalar1=s1, scalar2=None, op0=op0),
                          reads, writes)
            else:
                o_ = S.op(eng, lambda e: e.tensor_scalar(out=out, in0=in0, scalar1=s1, scalar2=s2, op0=op0, op1=op1),
                          reads, writes)
            o_.cost = ecost(eng, out)
            return o_

        def mm(out, lhsT, rhs, reads, writes, start=True, stop=True, skip=False):
            o_ = S.op("pe", lambda e: e.matmul(out, lhsT=lhsT, rhs=rhs, start=start, stop=stop,
                                               skip_group_check=skip), reads, writes)
            n_ = fsz(out)
            o_.cost = max(n_, 128) * 0.62 + 5.0
            if int(out.shape[0]) < 128:
                o_.cost = max(o_.cost, 160.0)
            if lhsT.dtype == F32:
                o_.cost = 4.0 * max(n_, 64) / 1.8 + 90.0
            return o_

        def tr(out, in_, reads, writes):
            o_ = S.op("pe", lambda e: e.transpose(out=out, in_=in_, identity=ID16[:]), list(reads) + ["ID16"], writes)
            o_.cost = 82.0
            return o_

        def bc4(ap4):
            return ap4.unsqueeze(2).to_broadcast([128, 4, 128])

        def v4(ap512):
            return ap512.rearrange("p (h d) -> p h d", h=4)

        PSF = [psum(f"psf{i}", [128, 512], F32) for i in range(8)]
        pools = cfg.get("pools", {"F": [0, 1, 2, 3], "B": [4, 5], "C": [6, 7]})
        pcnt = {"F": 0, "B": 0, "C": 0}
        stage_ = ["F"]

        def _next_bank():
            ids = pools[stage_[0]]
            i = ids[pcnt[stage_[0]] % len(ids)]
            pcnt[stage_[0]] += 1
            key = f"psf{i}"
            if cfg.get("inf_psum"):
                key = f"psf{i}_u{pcnt[stage_[0]]}{stage_[0]}"
            return i, key

        def pf():
            i, key = _next_bank()
            return PSF[i], key

        def pb():
            i, key = _next_bank()
            return PSF[i][:, :].bitcast(BF16), key

        WIN16 = sb("WIN16", [128, 8, DIN], BF16)
        WOUT16 = sb("WOUT16", [128, 8, D], BF16)
        POOLW16 = sb("POOLW16", [128, 4, 128], BF16)
        DIAG16 = sb("DIAG16", [128, 48, 128], BF16)
        NWC = sb("NWC", [128, 8])
        FNW = sb("FNW", [128, D])
        DNWC = sb("DNWC", [128, 1])
        PSROW = sb("PSROW", [128, 512])
        CONVW = sb("CONVW", [128, 48])
        ALOG = sb("ALOG", [128, 4])
        DTB = sb("DTB", [128, 4])
        NEGA = sb("NEGA", [128, 4])
        EPSC = sb("EPSC", [128, 1])
        ONEC = sb("ONEC", [128, 1])
        ONES16 = sb("ONES16", [128, 1], BF16)
        ONES128 = sb("ONES128", [128, 128])
        STAGED = {"ms": "MS16", "mu": "MU16", "wct": "WCT16", "wpt": "WPT16", "wc0hi": "WC0HI16", "wc0lo": "WC0LO16"}
        C32 = {k: sb("C_" + k, s) for k, s in CONST_SHAPES.items() if k not in STAGED}
        ID16 = sb("ID16", [128, 128], BF16)
        MS16 = sb("MS16", [128, 512], BF16)
        MU16 = sb("MU16", [128, 512], BF16)
        WCT16 = sb("WCT16", [128, 512], BF16)
        WPT16 = sb("WPT16", [128, 512], BF16)
        WC0HI16 = sb("WC0HI16", [128, 512], BF16)
        WC0LO16 = sb("WC0LO16", [128, 512], BF16)

        X32 = [sb(f"X32_{i}", [128, D]) for i in range(3)]
        H32 = sb("H32", [128, D])
        OUT32 = [sb(f"OUT32_{i}", [128, D]) for i in range(2)]
        XN16 = sb("XN16", [128, D], BF16)
        XT16 = sb("XT16", [128, 8, 128], BF16)
        RAW16 = [sb(f"RAW16_{i}", [128, 12, 131], BF16) for i in range(2)]
        QKV16 = sb("QKV16", [128, 12, 128], BF16)
        SQ16 = sb("SQ16", [128, 8, 128], BF16)
        SZ16 = sb("SZ16", [128, 512], BF16)
        U16 = [sb(f"U16_{i}", [128, 512], BF16) for i in range(2)]
        MIX16 = sb("MIX16", [128, 512], BF16)
        NHALF = sb("NHALF", [128, 8])
        GU = sb("GU", [128, 512])
        DA = sb("DA", [128, 512])
        EE = sb("EE16", [128, 512], BF16)
        AROW = sb("AROW", [128, 512], BF16)
        GSB = sb("GSB", [128, 512], BF16)
        GT = sb("GT", [128, 512], BF16)
        SMF = [sb(f"SMF_{i}", [128, 128]) for i in range(3)]
        GATE = [sb(f"GATE_{i}", [128, 512], BF16) for i in range(3)]
        KBA16 = [sb(f"KBA16_{i}", [128, 512], BF16) for i in range(2)]
        KDEC16 = [sb(f"KDEC16_{i}", [128, 512], BF16) for i in range(3)]
        VB16 = [sb(f"VB16_{i}", [128, 512], BF16) for i in range(2)]
        QKT16 = [sb(f"QKT16_{i}", [128, 512], BF16) for i in range(3)]
        QDT16 = [sb(f"QDT16_{i}", [128, 512], BF16) for i in range(3)]
        PB0 = [sb(f"PB0_{i}", [128, 512], BF16) for i in range(2)]
        PH0 = [sb(f"PH0_{i}", [128, 4, 128], BF16) for i in range(2)]
        PH1 = [sb(f"PH1_{i}", [128, 4, 256], BF16) for i in range(2)]
        YTP16 = [sb(f"YTP16_{i}", [128, 512], BF16) for i in range(3)]
        SMC = sb("SMC", [128, 32])
        PBX = [sb(f"PBX_{i}", [128, 512], BF16) for i in range(2)]
        PHX = [sb(f"PHX_{i}", [128, 4, 256], BF16) for i in range(2)]
        HF16 = sb("HF16", [128, 512], BF16)
        U32 = [sb(f"U32_{i}", [128, 512]) for i in range(2)]
        WT16 = [sb(f"WT16_{i}", [128, 512], BF16) for i in range(2)]
        VNZ = [sb(f"VNZ_{i}", [128, 512], BF16) for i in range(2)]
        S32 = sb("S32", [128, 512])
        S16 = sb("S16", [128, 512], BF16)
        ORAW = sb("ORAW", [128, 512])
        SQO = sb("SQO", [128, 512])
        YDN16 = sb("YDN16", [128, 512], BF16)
        YTD16 = sb("YTD16", [128, 512], BF16)

        def load(dst_ap, src_ap, key, wkeys):
            sem(key)
            S.dma("sp", dst_ap, src_ap, [], wkeys, key)

        for k in CONST_SHAPES:
            if k not in STAGED:
                load(C32[k][:], cd[k], "ld_c_" + k, ["C_" + k])
        load(NWC[:], nw_d, "ld_nw", ["NWC"])
        load(FNW[:], fnw_d.partition_broadcast(128), "ld_fnw", ["FNW"])
        load(DNWC[:], dnwc_d, "ld_dnwc", ["DNWC"])
        load(PSROW[:], psrow_d.partition_broadcast(128), "ld_psrow", ["PSROW"])
        load(ALOG[:], alog_d.partition_broadcast(128), "ld_alog", ["ALOG"])
        load(DTB[:], dtb_d.partition_broadcast(128), "ld_dtb", ["DTB"])
        load(CONVW[:], convw_d, "ld_convw", ["CONVW"])
        S.op("pool", lambda e: e.memset(EPSC[:], EPS), [], ["EPSC"])
        S.op("pool", lambda e: e.memset(ONEC[:], 1.0), [], ["ONEC"])
        S.op("pool", lambda e: e.memset(ONES128[:], 1.0), [], ["ONES128"])
        S.op("pool", lambda e: e.memset(NHALF[:], -0.5), [], ["NHALF"])
        cp("dve", ONES16[:], ONEC[:], ["ONEC"], ["ONES16"])
        cp("dve", ID16[:], C32["ident32"][:], ["C_ident32"], ["ID16"])
        act(NEGA[:], ALOG[:], AF.Exp, ["ALOG"], ["NEGA"])
        ts("dve", NEGA[:], NEGA[:], -1.0, None, ALU.mult, None, ["NEGA"], ["NEGA"])
        for ci in range(48):
            ts("dve", DIAG16[:, ci, :], C32["ident32"][:], CONVW[:, ci:ci + 1], None, ALU.mult, None,
               ["C_ident32", "CONVW"], [f"DIAG16_{ci}"])
        stage_slots = [(X32[0], "X32_0"), (X32[1], "X32_1"), (X32[2], "X32_2"), (H32, "H32"), (OUT32[0], "OUT32_0"), (OUT32[1], "OUT32_1")]
        dq = ["sp", "act"]
        dqi = [0]

        def dma_q():
            q_ = dq[dqi[0] % len(dq)]
            dqi[0] += 1
            return q_
        sidx = [0]

        def stage():
            s = stage_slots[sidx[0] % len(stage_slots)]
            sidx[0] += 1
            return s

        cv_eng = ["dve", "dve"]
        cvi = [0]

        def conv_eng():
            e = cv_eng[cvi[0] % 2]
            cvi[0] += 1
            return e

        STAGED_T = {"MS16": MS16, "MU16": MU16, "WCT16": WCT16, "WPT16": WPT16, "WC0HI16": WC0HI16, "WC0LO16": WC0LO16}
        for k, tname in STAGED.items():
            st, stk = stage()
            sem("ld_" + stk)
            S.dma("sp", st[:, 0:512], cd[k], [], [stk], "ld_" + stk)
            cp("dve", STAGED_T[tname][:], st[:, 0:512], [stk], [tname])
        st, stk = stage()
        sem("ld_" + stk)
        S.dma("sp", st[:, 0:512].rearrange("p (g d) -> p g d", g=4), poolw_d, [], [stk], "ld_" + stk)
        tt("dve", POOLW16[:].rearrange("p g d -> p (g d)"), st[:, 0:512], PSROW[:], ALU.mult, [stk, "PSROW"], ["POOLW16"])
        win_v = win_d.rearrange("(kc p) n -> p kc n", p=128)
        for kc in range(8):
            for (c0, cw) in ((0, 1024), (1024, 1024), (2048, 1024), (3072, 8)):
                st, stk = stage()
                sem("ld_" + stk)
                S.dma(dma_q(), st[:, 0:cw], win_v[:, kc, c0:c0 + cw], [], [stk], "ld_" + stk)
                ts("dve", WIN16[:, kc, c0:c0 + cw], st[:, 0:cw], NWC[:, kc:kc + 1], None, ALU.mult, None, [stk, "NWC"], ["WIN16"])
        wout_v = wout_d.rearrange("(kc p) n -> p kc n", p=128)
        for kc in range(8):
            st, stk = stage()
            sem("ld_" + stk)
            S.dma(dma_q(), st[:, :], wout_v[:, kc, :], [], [stk], "ld_" + stk)
            if kc < 4:
                cp(conv_eng(), WOUT16[:, kc, :], st[:, :], [stk], ["WOUT16"])
            else:
                ts("dve", WOUT16[:, kc, :], st[:, :], DNWC[:, 0:1], None, ALU.mult, None, [stk, "DNWC"], ["WOUT16"])
        S.op("pool", lambda e: e.memset(VNZ[0][:], 0.0), [], ["VNZ_0"])
        S.op("pool", lambda e: e.memset(VNZ[1][:], 0.0), [], ["VNZ_1"])
        S.op("pool", lambda e: e.memset(S32[:], 0.0), [], ["S32"])
        S.op("pool", lambda e: e.memset(S16[:], 0.0), [], ["S16"])
        S.op("pool", lambda e: e.memset(RAW16[1][:], 0.0), [], ["RAW16_1_0", "RAW16_1_1", "RAW16_1_2", "RAW16_1_h"])

        x_v = x_d.rearrange("(n p) d -> n p d", p=128)
        out_v = out_d.rearrange("(n p) d -> n p d", p=128)
        for k in ("ld_X32_0", "ld_X32_1", "ld_X32_2", "st_OUT32_0", "st_OUT32_1"):
            sem(k)

        def rsqrt_small(out, in_, reads, writes, scale_in=1.0):
            n_ = fsz(out)
            ts("dve", out, in_, float(scale_in), EPS, ALU.mult, ALU.add, reads, writes)
            tt("pool", out, out, NHALF[:, 0:n_], ALU.pow, list(writes) + ["NHALF"], writes)

        def load_x(t):
            b3 = t % 3
            S.dma("sp", X32[b3][:], x_v[t], [], [f"X32_{b3}"], f"ld_X32_{b3}")

        def front(t):
            stage_[0] = "F"
            b = t % 2
            X, xk = X32[t % 3], f"X32_{t % 3}"
            b3 = t % 3
            SM = SMF[b3]
            smk = lambda n: f"SMF{b3}_{n}"
            act(XN16[:], X[:], AF.Square, [xk], ["XN16", smk("ssx")], accum=SM[:, 0:1])
            rsqrt_small(SM[:, 1:2], SM[:, 0:1], [smk("ssx")], [smk("rx")], scale_in=1.0 / D)
            ts("dve", XN16[:], X[:], SM[:, 1:2], None, ALU.mult, None, [xk, smk("rx")], ["XN16"])
            P, pk = pb()
            for kc in range(8):
                tr(P[:, kc * 128:(kc + 1) * 128], XN16[:, kc * 128:(kc + 1) * 128], ["XN16"], [pk])
            cp("act", XT16[:].rearrange("p k t -> p (k t)"), P[:, :], [pk], ["XT16"])
            S.cut()
            R, rk_ = RAW16[b], f"RAW16_{b}"
            Rp, rpk = RAW16[1 - b], f"RAW16_{1 - b}"
            cp("pool", R[:, :, 0:3], Rp[:, :, 128:131], [rpk + "_0", rpk + "_1", rpk + "_2"], [rk_ + "_h"])
            for grp in range(3):
                P, pk = pf()
                for ci in range(4):
                    c0 = C_Q + (grp * 4 + ci) * 128
                    for kc in range(8):
                        mm(P[:, ci * 128:(ci + 1) * 128], WIN16[:, kc, c0:c0 + 128], XT16[:, kc, :], ["WIN16", "XT16"], [pk],
                           start=(kc == 0), stop=(kc == 7))
                    if ci % 2 == 1:
                        S.cut()
                cp("dve" if grp == 0 else "act", R[:, grp * 4:(grp + 1) * 4, 3:131], v4(P[:, :]), [pk], [rk_ + f"_{grp}"])
            P, pk = pf()
            for ci in range(4):
                c0 = C_PZ + ci * 128
                for kc in range(8):
                    mm(P[:, ci * 128:(ci + 1) * 128], WIN16[:, kc, c0:c0 + 128], XT16[:, kc, :], ["WIN16", "XT16"], [pk],
                       start=(kc == 0), stop=(kc == 7))
                if ci % 2 == 1:
                    S.cut()
            act(SZ16[:], P[:, :], AF.Silu, [pk], ["SZ16"])
            P, pk = pf()
            for kc in range(8):
                mm(P[:, 0:512], XT16[:, kc, :], WIN16[:, kc, C_PU:C_PU + 512], ["WIN16", "XT16"], [pk],
                   start=(kc == 0), stop=(kc == 7))
            S.cut()
            Uc, uk = U16[b], f"U16_{b}"
            Up, upk = U16[1 - b], f"U16_{1 - b}"
            cp("dve", Uc[:], P[:, :], [pk], [uk])
            P, pk = pf()
            for kc in range(8):
                mm(P[:, 0:512], XT16[:, kc, :], WIN16[:, kc, C_DZ:C_DZ + 512], ["WIN16", "XT16"], [pk],
                   start=(kc == 0), stop=(kc == 7))
            S.cut()
            act(GATE[b3][:], P[:, :], AF.Silu, [pk], [f"GATE_{b3}"])
            P, pk = pf()
            for kc in range(8):
                mm(P[:, 0:8], XT16[:, kc, :], WIN16[:, kc, C_BA:C_BA + 8], ["WIN16", "XT16"], [pk],
                   start=(kc == 0), stop=(kc == 7))
            cp("dve", SM[:, 8:16], P[:, 0:8], [pk], [smk("ba")])
            S.cut()
            for grp in range(3):
                P, pk = pf()
                for ci in range(4):
                    ch = grp * 4 + ci
                    for j in range(4):
                        mm(P[:, ci * 128:(ci + 1) * 128], DIAG16[:, ch * 4 + j, :], R[:, ch, j:j + 128], [f"DIAG16_{ch * 4 + j}", rk_ + "_h", rk_ + f"_{grp}"], [pk],
                           start=(j == 0), stop=(j == 3))
                act(QKV16[:, grp * 4:(grp + 1) * 4, :].rearrange("p c t -> p (c t)"), P[:, :], AF.Silu, [pk], [f"QKV16_{grp}"])
                S.cut()
            Pm, pkm = pf()
            for g in range(4):
                gs = slice(g * 128, (g + 1) * 128)
                if t == 0:
                    mm(Pm[:, gs], Uc[:, gs], WCT16[:, gs], [uk, "WCT16"], [pkm], start=True, stop=False)
                    mm(Pm[:, gs], Uc[:, gs], WC0HI16[:, gs], [uk, "WC0HI16"], [pkm], start=False, stop=False)
                    mm(Pm[:, gs], Uc[:, gs], WC0LO16[:, gs], [uk, "WC0LO16"], [pkm], start=False, stop=True)
                else:
                    mm(Pm[:, gs], Up[:, gs], WPT16[:, gs], [upk, "WPT16"], [pkm], start=True, stop=False)
                    mm(Pm[:, gs], Uc[:, gs], WCT16[:, gs], [uk, "WCT16"], [pkm], start=False, stop=True)
            cp("act", MIX16[:], Pm[:, :], [pkm], ["MIX16"])
            S.cut()
            Pp, pkp = pf()
            for g in range(4):
                gs = slice(g * 128, (g + 1) * 128)
                mm(Pp[:, gs], POOLW16[:, g, :], MIX16[:, gs], ["POOLW16", "MIX16"], [pkp])
            tt("dve", YTP16[b3][:], Pp[:, :], SZ16[:], ALU.mult, [pkp, "SZ16"], [f"YTP16_{b3}"])
            S.cut()
            tt("dve", SQ16[:], QKV16[:, 0:8, :], QKV16[:, 0:8, :], ALU.mult, ["QKV16_0", "QKV16_1"], ["SQ16"])
            P, pk = pf()
            for c in range(8):
                mm(P[:, c:c + 1], SQ16[:, c, :], ONES16[:, 0:1], ["SQ16", "ONES16"], [pk])
            cp("dve", SM[:, 16:24], P[:, 0:8], [pk], [smk("ssqk")])
            S.cut()
            rsqrt_small(SM[:, 24:32], SM[:, 16:24], [smk("ssqk")], [smk("rqk")])
            ts("dve", SM[:, 24:28], SM[:, 24:28], float(128 ** -0.5), None, ALU.mult, None, [smk("rqk")], [smk("rqk")])
            act(SM[:, 32:36], SM[:, 8:12], AF.Exp, [smk("ba")], [smk("beta")], scale=-1.0)
            ts("dve", SM[:, 32:36], SM[:, 32:36], 1.0, None, ALU.add, None, [smk("beta")], [smk("beta")])
            S.op("dve", lambda e: e.reciprocal(out=SM[:, 32:36], in_=SM[:, 32:36]), [smk("beta")], [smk("beta")])
            tt("dve", SM[:, 36:40], SM[:, 12:16], DTB[:, 0:4], ALU.add, [smk("ba"), "DTB"], [smk("g")])
            act(SM[:, 36:40], SM[:, 36:40], AF.Exp, [smk("g")], [smk("g")])
            act(SM[:, 36:40], SM[:, 36:40], AF.Ln, [smk("g"), "ONEC"], [smk("g")], bias=ONEC[:, 0:1])
            tt("dve", SM[:, 36:40], SM[:, 36:40], NEGA[:, 0:4], ALU.mult, [smk("g"), "NEGA"], [smk("g")])
            P, pk = pf()
            mm(P[:, 0:4], C32["u2"][:], SM[:, 36:40], ["C_u2", smk("g")], [pk])
            mm(P[:, 4:8], C32["cisame"][:], SM[:, 36:40], ["C_cisame", smk("g")], [pk])
            mm(P[:, 8:12], C32["ci0"][:], SM[:, 36:40], ["C_ci0", smk("g")], [pk])
            mm(P[:, 12:16], C32["ci1"][:], SM[:, 36:40], ["C_ci1", smk("g")], [pk])
            cp("dve", SM[:, 40:56], P[:, 0:16], [pk], [smk("gc"), smk("gc2")])
            tt("dve", SM[:, 44:48], SM[:, 44:48], SM[:, 40:44], ALU.subtract, [smk("gc")], [smk("gc2")])
            act(SM[:, 56:72], SM[:, 40:56], AF.Exp, [smk("gc"), smk("gc2")], [smk("ex")])
            RK = SM[:, 28:32]
            BETA = SM[:, 32:36]
            tt("dve", SM[:, 84:88], RK, BETA, ALU.mult, [smk("rqk"), smk("beta")], [smk("c0")])
            stt("dve", SM[:, 72:76], SM[:, 84:88], -1.0, RK, ALU.mult, ALU.mult, [smk("c0"), smk("rqk")], [smk("c1")])
            stt("dve", SM[:, 76:80], SM[:, 72:76], -1.0, SM[:, 56:60], ALU.mult, ALU.mult, [smk("c1"), smk("ex")], [smk("c2")])
            S.cut()
            Pk, pkk = pb()
            for h in range(4):
                tr(Pk[:, h * 128:(h + 1) * 128], QKV16[:, 4 + h, :], ["QKV16_1"], [pkk])
                tr(Pk[:, 512 + h * 128:512 + (h + 1) * 128], QKV16[:, 8 + h, :], ["QKV16_2"], [pkk])
            tt("dve", v4(KDEC16[b3][:]), v4(Pk[:, 0:512]), bc4(SM[:, 60:64]), ALU.mult, [pkk, smk("ex")], [f"KDEC16_{b3}"])
            tt("dve", v4(KBA16[b][:]), v4(Pk[:, 0:512]), bc4(SM[:, 76:80]), ALU.mult, [pkk, smk("c2")], [f"KBA16_{b}"])
            tt("dve", v4(VB16[b][:]), v4(Pk[:, 512:1024]), bc4(SM[:, 84:88]), ALU.mult, [pkk, smk("c0")], [f"VB16_{b}"])
            S.cut()
            tt("dve", v4(GU[:]), v4(C32["u2rep"][:]), bc4(SM[:, 36:40]), ALU.mult, ["C_u2rep", smk("g")], ["GU"])
            Pg, pkg = pf()
            mm(Pg[:, :], ONES128[:], GU[:], ["ONES128", "GU"], [pkg])
            ts("dve", SM[:, 88:92], SM[:, 40:44], -1.0, None, ALU.mult, None, [smk("gc")], [smk("ngc")])
            for h in range(4):
                act(DA[:, h * 128:(h + 1) * 128], Pg[:, h * 128:(h + 1) * 128], AF.Abs, [pkg, smk("ngc")], ["DA"],
                    bias=SM[:, 88 + h:89 + h])
            act(AROW[:], Pg[:, :], AF.Exp, [pkg], ["AROW"])
            act(EE[:], DA[:], AF.Exp, ["DA"], ["EE16"], scale=-1.0)
            S.cut()
            tt("dve", GT[:], EE[:], MU16[:], ALU.mult, ["EE16", "MU16"], ["GT"])
            tt("dve", GSB[:], EE[:], MS16[:], ALU.mult, ["EE16", "MS16"], ["GSB"])
            tt("dve", QDT16[b3][:], QKV16[:, 0:4, :].rearrange("p c t -> p (c t)"), AROW[:], ALU.mult, ["QKV16_0", "AROW"], [f"QDT16_{b3}"])
            Pkk, pkkk = pf()
            for h in range(4):
                mm(Pkk[:, h * 128:(h + 1) * 128], QKV16[:, 4 + h, :], QKV16[:, 4 + h, :], ["QKV16_1"], [pkkk])
            Pqk, pkqk = pf()
            for h in range(4):
                mm(Pqk[:, h * 128:(h + 1) * 128], QKV16[:, 4 + h, :], QKV16[:, h, :], ["QKV16_1", "QKV16_0"], [pkqk])
            S.cut()
            for h in range(4):
                hs = slice(h * 128, (h + 1) * 128)
                stt("dve", PB0[b][:, hs], Pkk[:, hs], SM[:, 72 + h:73 + h], GSB[:, hs], ALU.mult, ALU.mult,
                    [pkkk, "GSB", smk("c1")], [f"PB0_{b}"])
            tt("dve", QKT16[b3][:], Pqk[:, :], GT[:], ALU.mult, [pkqk, "GT"], [f"QKT16_{b3}"])
            P3, pk3 = pb()
            for h in range(4):
                tr(P3[:, h * 128:(h + 1) * 128], PB0[b][:, h * 128:(h + 1) * 128], [f"PB0_{b}"], [pk3])
            cp("act", PH0[b][:, :, :], v4(P3[:, 0:512]), [pk3], [f"PH0_{b}"])
            tt("dve", PH1[b][:, :, 128:256], v4(P3[:, 0:512]), ID16[:].unsqueeze(1).to_broadcast([128, 4, 128]), ALU.add,
               [pk3, "ID16"], [f"PH1_{b}_H"])
            S.cut()

        def stageB(t):
            stage_[0] = "B"
            b = t % 2
            X, xk = X32[t % 3], f"X32_{t % 3}"
            SM = SMF[b]
            smk = lambda n: f"SMF{b}_{n}"
            seq = [(PB0[b], f"PB0_{b}", PH0[b], f"PH0_{b}", None), (PBX[0], "PBX_0", PH1[b], f"PH1_{b}_P", f"PH1_{b}_H"),
                   (PBX[1], "PBX_1", PHX[1], "PHX_1_P", "PHX_1_H"), (PBX[0], "PBX_0", PHX[0], "PHX_0_P", "PHX_0_H"),
                   (PBX[1], "PBX_1", PHX[1], "PHX_1_P", "PHX_1_H"), (PBX[0], "PBX_0", PHX[0], "PHX_0_P", "PHX_0_H")]
            for k in range(5):
                PBc, kb, PHc, khP, khH = seq[k]
                PBn, kbn, PHn, khnP, khnH = seq[k + 1]
                Pa, pka = pf()
                for h in range(4):
                    mm(Pa[:, h * 128:(h + 1) * 128], PHc[:, h, 0:128], PBc[:, h * 128:(h + 1) * 128], [khP, kb], [pka])
                cp("act", PBn[:], Pa[:, :], [pka], [kbn])
                if k == 0:
                    Pb_, pkb = pf()
                    for h in range(4):
                        mm(Pb_[:, h * 128:(h + 1) * 128], PBc[:, h * 128:(h + 1) * 128], PHc[:, h, 0:128], [kb, khP], [pkb])
                    cp("dve", PHn[:, :, 0:128], v4(Pb_[:, :]), [pkb], [khnP])
                elif k < 4:
                    Pb0, pkb0 = pf()
                    Pb1, pkb1 = pf()
                    for h in range(4):
                        Pd = Pb0 if h < 2 else Pb1
                        pkd = pkb0 if h < 2 else pkb1
                        mm(Pd[:, (h % 2) * 256:(h % 2) * 256 + 256], PBc[:, h * 128:(h + 1) * 128], PHc[:, h, :], [kb, khP, khH], [pkd])
                    for half, (Pd, pkd) in enumerate(((Pb0, pkb0), (Pb1, pkb1))):
                        pv = Pd[:, :].rearrange("p (h c) -> p h c", h=2)
                        hs = slice(half * 2, half * 2 + 2)
                        cp("act", PHn[:, hs, 0:128], pv[:, :, 0:128], [pkd], [khnP])
                        tt("dve", PHn[:, hs, 128:256], pv[:, :, 128:256], PHc[:, hs, 128:256], ALU.add, [pkd, khH], [khnH])
                else:
                    Pb_, pkb = pf()
                    for h in range(4):
                        mm(Pb_[:, h * 128:(h + 1) * 128], PBc[:, h * 128:(h + 1) * 128], PHc[:, h, 128:256], [kb, khH], [pkb])
                    tt("dve", PHn[:, :, 128:256], v4(Pb_[:, :]), PHc[:, :, 128:256], ALU.add, [pkb, khH], [khnH])
                S.cut()
            PBc, kb, PHc, khP, khH = seq[5]
            Pb_, pkb = pf()
            for h in range(4):
                mm(Pb_[:, h * 128:(h + 1) * 128], PBc[:, h * 128:(h + 1) * 128], PHc[:, h, 128:256], [kb, khH], [pkb])
            tt("dve", v4(HF16[:]), v4(Pb_[:, :]), PHc[:, :, 128:256], ALU.add, [pkb, khH], ["HF16"])
            S.cut()
            Pu, pku = pf()
            for h in range(4):
                mm(Pu[:, h * 128:(h + 1) * 128], HF16[:, h * 128:(h + 1) * 128], VB16[b][:, h * 128:(h + 1) * 128], ["HF16", f"VB16_{b}"], [pku])
            cp("act", U32[b][:], Pu[:, :], [pku], [f"U32_{b}"])
            Pw, pkw = pf()
            for h in range(4):
                mm(Pw[:, h * 128:(h + 1) * 128], KBA16[b][:, h * 128:(h + 1) * 128], HF16[:, h * 128:(h + 1) * 128], [f"KBA16_{b}", "HF16"], [pkw])
            cp("dve", WT16[b][:], Pw[:, :], [pkw], [f"WT16_{b}"])
            S.cut()
        def stageC(t):
            stage_[0] = "C"
            b = t % 2
            b3 = t % 3
            X, xk = X32[t % 3], f"X32_{t % 3}"
            SM = SMF[b3]
            smk = lambda n: f"SMF{b3}_{n}"
            for c in range(2):
                rs = slice(64 * c, 64 * c + 64)
                VN, vnk = VNZ[c], f"VNZ_{c}"
                Pr, pkr = pf()
                for h in range(4):
                    hs = slice(h * 128, (h + 1) * 128)
                    mm(Pr[:, hs], WT16[b][:, hs], S16[:, hs], [f"WT16_{b}", "S16"], [pkr])
                tt("dve", VN[rs, :], U32[b][rs, :], Pr[rs, :], ALU.subtract, [f"U32_{b}", pkr], [vnk])
                S.cut()
                Po, pko = pf()
                for h in range(4):
                    hs = slice(h * 128, (h + 1) * 128)
                    mm(Po[:, hs], QDT16[b3][:, hs], S16[:, hs], [f"QDT16_{b3}", "S16"], [pko], start=True, stop=False)
                    mm(Po[:, hs], QKT16[b3][:, hs], VN[:, hs], [f"QKT16_{b3}", vnk], [pko], start=False, stop=True)
                cp("act", ORAW[rs, :], Po[rs, :], [pko], ["ORAW"])
                Ps, pks = pf()
                for h in range(4):
                    hs = slice(h * 128, (h + 1) * 128)
                    mm(Ps[:, hs], KDEC16[b3][:, hs], VN[:, hs], [f"KDEC16_{b3}", vnk], [pks])
                for h in range(4):
                    hs = slice(h * 128, (h + 1) * 128)
                    stt("dve", S32[:, hs], S32[:, hs], SM[:, 64 + 4 * c + h:65 + 4 * c + h], Ps[:, hs], ALU.mult, ALU.add,
                        ["S32", smk("ex"), pks], ["S32"])
                cp("act", S16[:], S32[:], ["S32"], ["S16"])
                S.cut()
            for h in range(4):
                act(SQO[:, h * 128:(h + 1) * 128], ORAW[:, h * 128:(h + 1) * 128], AF.Square, ["ORAW"], ["SQO", "SMC_sso"],
                    accum=SMC[:, h:h + 1])
            RQ = SM[:, 24:28]
            tt("dve", SMC[:, 4:8], RQ, RQ, ALU.mult, [smk("rqk")], ["SMC_o1"])
            stt("dve", SMC[:, 4:8], SMC[:, 4:8], 1.0 / 128.0, SMC[:, 0:4], ALU.mult, ALU.mult, ["SMC_o1", "SMC_sso"], ["SMC_o1"])
            rsqrt_small(SMC[:, 8:12], SMC[:, 4:8], ["SMC_o1"], ["SMC_o2"])
            tt("dve", SMC[:, 8:12], SMC[:, 8:12], RQ, ALU.mult, ["SMC_o2", smk("rqk")], ["SMC_o2"])
            for h in range(4):
                hs = slice(h * 128, (h + 1) * 128)
                stt("dve", YDN16[:, hs], ORAW[:, hs], SMC[:, 8 + h:9 + h], GATE[b3][:, hs], ALU.mult, ALU.mult,
                    ["ORAW", "SMC_o2", f"GATE_{b3}"], ["YDN16"])
            S.cut()
            P4, pk4 = pb()
            for h in range(4):
                tr(P4[:, h * 128:(h + 1) * 128], YDN16[:, h * 128:(h + 1) * 128], ["YDN16"], [pk4])
            cp("act", YTD16[:], P4[:, 0:512], [pk4], ["YTD16"])
            S.cut()
            for nh in range(2):
                Pq, pkq = pf()
                for e8 in range(8):
                    lhs = YTP16[b3][:, e8 * 128:(e8 + 1) * 128] if e8 < 4 else YTD16[:, (e8 - 4) * 128:(e8 - 3) * 128]
                    mm(Pq[:, 0:512], lhs, WOUT16[:, e8, nh * 512:(nh + 1) * 512],
                       [f"YTP16_{b3}" if e8 < 4 else "YTD16", "WOUT16"], [pkq], start=(e8 == 0), stop=(e8 == 7))
                tt("dve", H32[:, nh * 512:(nh + 1) * 512], Pq[:, :], X[:, nh * 512:(nh + 1) * 512], ALU.add, [pkq, xk], ["H32"])
            if t + 3 < n_tiles:
                load_x(t + 3)
            act(SQO[:].bitcast(BF16), H32[:], AF.Square, ["H32"], ["SQO", "SMC_ssh"], accum=SMC[:, 12:13])
            rsqrt_small(SMC[:, 13:14], SMC[:, 12:13], ["SMC_ssh"], ["SMC_rh"], scale_in=1.0 / D)
            O, ok = OUT32[b], f"OUT32_{b}"
            stt(cfg.get("out_eng", "dve"), O[:], H32[:], SMC[:, 13:14], FNW[:], ALU.mult, ALU.mult, ["H32", "SMC_rh", "FNW"], [ok])
            S.dma("sp", out_v[t], O[:], [ok], [], f"st_OUT32_{b}")
            S.cut()

        for t0 in range(min(3, n_tiles)):
            load_x(t0)
        for t in range(n_tiles):
            front(t)
            stageB(t)
            stageC(t)
        if cfg.get("junk"):
            g_, f_, m_ = cfg["junk"]

            jb_ = cfg.get("junk_bank", 7)

            def _mk_junk():
                o_ = _Op("pe", lambda e: e.matmul(PSF[jb_][:, 0:128], lhsT=ID16[:], rhs=ID16[:], start=True, stop=True,
                                                  skip_group_check=True), ["ID16"], [])
                return o_
            Sched.JUNK = (g_, f_, m_, _mk_junk)
        else:
            Sched.JUNK = None
        Sched.ACT_PEN = float(cfg.get("act_pen", 1400.0))
        Sched.JITTER = tuple(cfg["jitter"]) if cfg.get("jitter") else None
        Sched.ATTACH_WAIT = bool(cfg.get("attach_wait", True))
        Sched.XLAT = float(cfg.get("xlat", 715.0))
        S.schedule()
        print("[sched] est_total_us=%.1f n_ops=%d" % (S.est_total / 1e3, len(S.ops)))
        S.emit(sems)
    return nc, list(dbg_out.keys())


def _core_inputs(inputs, b, consts):
    f = lambda a: np.ascontiguousarray(np.asarray(a, dtype=np.float32))
    conv_w = np.asarray(inputs["conv_w"][0], np.float32)
    m = {
        "x": f(inputs["x"][b]),
        "w_in": f(inputs["w_in"][0]),
        "w_out": f(inputs["w_out"][0]),
        "pool_w": f(np.transpose(np.asarray(inputs["pool_w"][0]), (1, 0, 2))),
        "pool_scale": f(np.asarray(inputs["pool_scale"][0]).reshape(512)),
        "conv_w": f(np.transpose(conv_w.reshape(4, 12, 128), (2, 1, 0)).reshape(128, 48)),
        "a_log": f(inputs["a_log"][0]),
        "dt_bias": f(inputs["dt_bias"][0]),
        "dn_norm_w": f(np.asarray(inputs["dn_norm_w"][0]).reshape(128, 1)),
        "norm_w": f(np.asarray(inputs["norm_w"][0]).reshape(8, 128).T),
        "final_norm_w": f(inputs["final_norm_w"]),
    }
    for k, v in consts.items():
        m["c_" + k] = f(v)
    return m


def kernel(**inputs):
    consts = _consts()
    nc, _ = build_nc()
    in_maps = [_core_inputs(inputs, b, consts) for b in range(8)]
    res = run_bass_kernel_spmd(nc, in_maps, core_ids=list(range(8)))
    out = np.stack([np.asarray(r["out"], dtype=np.float32) for r in res.results], axis=0)
    return out
```
